# Optimizing a Trainium2 kernel written in Bass

```python
import math
import jax, jax.numpy as jnp
from jax import lax
import numpy as np

D_MODEL = 1024
BATCH = 8
SEQ = 2048
DEPTH = 2
DEC_BATCH = 8
DEC_SEQ = 16
PAST_LEN = 2048

CHUNK = 64
N_EVEN = (DEPTH + 1) // 2
N_ODD = DEPTH // 2
MLP_CHUNK = 128
A_GROUPS = 4
A_WIDTH = D_MODEL // 2
A_GC = A_WIDTH // A_GROUPS
POOL_WINDOWS = (2, 4, 8, 16)
B_WIDTH = D_MODEL // 2
B_GC = B_WIDTH // len(POOL_WINDOWS)
POOL_PAD = max(POOL_WINDOWS) - 1
IN_AB = 2 * A_WIDTH + B_WIDTH
OUT_AB = A_WIDTH + B_WIDTH
SB_HEADS = 16
SB_HEAD_DIM = D_MODEL // SB_HEADS
Q_BLOCK = 128
D_FF = -(-8 * D_MODEL // (3 * 256)) * 256
EPS = 1e-6

kernel_name = "stream_gmlp_pool_stickbreak_step"


def _chunk_mask(L):
    i = np.arange(L)
    return (i[None, :] // CHUNK) <= (i[:, None] // CHUNK)


def rms_norm(x, g):
    xf = x.astype(jnp.float32)
    y = xf * lax.rsqrt(jnp.mean(xf * xf, axis=-1, keepdims=True) + EPS)
    return (y * g.astype(jnp.float32)).astype(x.dtype)


def layer_norm(x, g, b):
    xf = x.astype(jnp.float32)
    mu = jnp.mean(xf, axis=-1, keepdims=True)
    xc = xf - mu
    y = xc * lax.rsqrt(jnp.mean(xc * xc, axis=-1, keepdims=True) + EPS)
    return (y * g.astype(jnp.float32) + b.astype(jnp.float32)).astype(x.dtype)


def swiglu(x, wg, wu, wd):
    return (jax.nn.silu(x @ wg) * (x @ wu)) @ wd


def spatial_gating(u, v_n, w_s, b_s, L):
    B, T, _ = v_n.shape
    n = T // L
    w = jnp.where(_chunk_mask(L)[None], w_s[:, :L, :L], 0.0)
    vv = v_n.reshape(B, n, L, A_GROUPS, A_GC)
    s = jnp.einsum('gij,bnjgc->bnigc', w, vv) + b_s[:, :L].T[None, None, :, :, None]
    return u * s.reshape(B, T, A_WIDTH)


def pool_mixer(p, prev, start_pos, w_map, scale):
    B, T, C = p.shape
    full = jnp.concatenate([prev, p], axis=1)
    ff = full.astype(jnp.float32)
    cs = jnp.concatenate([jnp.zeros((B, 1, C), jnp.float32), jnp.cumsum(ff, axis=1)], axis=1)
    end = cs[:, POOL_PAD + 1:]
    pos = start_pos + jnp.arange(T)
    pf = p.astype(jnp.float32)
    outs = []
    for g, w in enumerate(POOL_WINDOWS):
        sl = slice(g * B_GC, (g + 1) * B_GC)
        win = end[..., sl] - cs[:, POOL_PAD + 1 - w:POOL_PAD + 1 - w + T, sl]
        cnt = jnp.minimum(w, pos + 1).astype(jnp.float32)
        outs.append(win / cnt[None, :, None] - pf[..., sl])
    d = jnp.stack(outs, axis=2)
    y = jnp.einsum('btgc,gce->btge', d, w_map.astype(jnp.float32)).reshape(B, T, C)
    y = (y * scale.astype(jnp.float32)).astype(p.dtype)
    return y, full[:, -POOL_PAD:]


def mixer_ab(h, prev_pool, start_pos, L, w_in, ln_g, ln_b, w_s, b_s, w_map, p_scale, w_out):
    z = h @ w_in
    uv = jax.nn.gelu(z[..., :2 * A_WIDTH], approximate=False)
    u, v = uv[..., :A_WIDTH], uv[..., A_WIDTH:]
    v_n = layer_norm(v, ln_g, ln_b)
    a_out = spatial_gating(u, v_n, w_s, b_s, L)
    b_out, tail = pool_mixer(z[..., 2 * A_WIDTH:], prev_pool, start_pos, w_map, p_scale)
    y = jnp.concatenate([a_out, b_out], axis=-1) @ w_out
    return y, tail, v_n


def stick_breaking(q, k, v, q_pos, k_pos):
    z = jnp.einsum('bqhd,bkhd->bhqk', q.astype(jnp.float32), k.astype(jnp.float32)) * (SB_HEAD_DIM ** -0.5)
    causal = (k_pos[None, :] < q_pos[:, None])[None, None]
    log_1m = jnp.where(causal, jax.nn.log_sigmoid(-z), 0.0)
    after = lax.cumsum(log_1m, axis=3, reverse=True) - log_1m
    a = jnp.where(causal, jnp.exp(jax.nn.log_sigmoid(z) + after), 0.0)
    return jnp.einsum('bhqk,bkhd->bqhd', a, v.astype(jnp.float32)).astype(q.dtype)


def stick_breaking_prompt(q, k, v):
    B, S, H, Dh = q.shape
    nb = S // Q_BLOCK
    qb = q.reshape(B, nb, Q_BLOCK, H, Dh).transpose(1, 0, 2, 3, 4)
    qpos = jnp.arange(S).reshape(nb, Q_BLOCK)
    kpos = jnp.arange(S)
    ob = lax.map(lambda xs: stick_breaking(xs[0], k, v, xs[1], kpos), (qb, qpos))
    return ob.transpose(1, 0, 2, 3, 4).reshape(B, S, H, Dh)


def mixer_c(h, k_prev, v_prev, w_qkv, w_o, is_prompt):
    B, T, _ = h.shape
    qkv = (h @ w_qkv).reshape(B, T, 3, SB_HEADS, SB_HEAD_DIM)
    q, k, v = qkv[:, :, 0], qkv[:, :, 1], qkv[:, :, 2]
    if is_prompt:
        o = stick_breaking_prompt(q, k, v)
    else:
        P = k_prev.shape[1]
        k_all = jnp.concatenate([k_prev, k], axis=1)
        v_all = jnp.concatenate([v_prev, v], axis=1)
        o = stick_breaking(q, k_all, v_all, P + jnp.arange(T), jnp.arange(P + T))
    return o.reshape(B, T, D_MODEL) @ w_o, k, v


def trunk(x, is_prompt, state_pool, cache_k, cache_v, p):
    B, T, _ = x.shape
    tails, ks, vs, vns = [], [], [], []
    for layer in range(DEPTH):
        i = layer // 2
        h = rms_norm(x, p['norm_mix_pre'][layer])
        if layer % 2 == 0:
            if is_prompt:
                prev, start, L = jnp.zeros((B, POOL_PAD, B_WIDTH), x.dtype), 0, MLP_CHUNK
            else:
                prev, start, L = state_pool[i], PAST_LEN, T
            y, tail, v_n = mixer_ab(h, prev, start, L, p['w_in_ab'][i], p['ln_v_g'][i], p['ln_v_b'][i],
                                    p['w_spatial'][i], p['b_spatial'][i], p['w_pool_map'][i],
                                    p['pool_scale'][i], p['w_out_ab'][i])
            tails.append(tail)
            vns.append(v_n)
        else:
            kp = None if is_prompt else cache_k[i]
            vp = None if is_prompt else cache_v[i]
            y, k, v = mixer_c(h, kp, vp, p['w_qkv'][i], p['w_o_sb'][i], is_prompt)
            ks.append(k)
            vs.append(v)
        x = x + rms_norm(y, p['norm_mix_post'][layer])
        h = rms_norm(x, p['norm_ffn_pre'][layer])
        f = swiglu(h, p['w_gate'][layer], p['w_up'][layer], p['w_down'][layer])
        x = x + rms_norm(f, p['norm_ffn_post'][layer])
    return x, jnp.stack(tails), jnp.stack(ks), jnp.stack(vs), jnp.stack(vns)


def setup_inputs(seed: int = 0) -> dict:
    key = jax.random.key(seed)
    k = jax.random.split(key, 24)
    f32 = jnp.float32
    nrm = lambda kk, shape, s: jax.random.normal(kk, shape, f32) * s
    gain = lambda kk, shape: 1.0 + nrm(kk, shape, 0.05)
    return {
        'x_prompt': nrm(k[0], (BATCH, SEQ, D_MODEL), 1.0),
        'x_sample': nrm(k[1], (DEC_BATCH, DEC_SEQ, D_MODEL), 1.0),
        'state_pool': nrm(k[2], (N_EVEN, DEC_BATCH, POOL_PAD, B_WIDTH), 1.0),
        'cache_k': nrm(k[3], (N_ODD, DEC_BATCH, PAST_LEN, SB_HEADS, SB_HEAD_DIM), 1.0),
        'cache_v': nrm(k[4], (N_ODD, DEC_BATCH, PAST_LEN, SB_HEADS, SB_HEAD_DIM), 1.0),
        'norm_mix_pre': gain(k[5], (DEPTH, D_MODEL)),
        'norm_mix_post': gain(k[6], (DEPTH, D_MODEL)),
        'norm_ffn_pre': gain(k[7], (DEPTH, D_MODEL)),
        'norm_ffn_post': gain(k[8], (DEPTH, D_MODEL)),
        'w_in_ab': nrm(k[9], (N_EVEN, D_MODEL, IN_AB), D_MODEL ** -0.5),
        'ln_v_g': gain(k[10], (N_EVEN, A_WIDTH)),
        'ln_v_b': nrm(k[11], (N_EVEN, A_WIDTH), 0.02),
        'w_spatial': nrm(k[12], (N_EVEN, A_GROUPS, MLP_CHUNK, MLP_CHUNK), MLP_CHUNK ** -0.5),
        'b_spatial': nrm(k[13], (N_EVEN, A_GROUPS, MLP_CHUNK), 0.02),
        'w_pool_map': nrm(k[14], (N_EVEN, len(POOL_WINDOWS), B_GC, B_GC), B_GC ** -0.5),
        'pool_scale': 1.0 + nrm(k[15], (N_EVEN, B_WIDTH), 0.1),
        'w_out_ab': nrm(k[16], (N_EVEN, OUT_AB, D_MODEL), OUT_AB ** -0.5),
        'w_qkv': nrm(k[17], (N_ODD, D_MODEL, 3 * D_MODEL), D_MODEL ** -0.5),
        'w_o_sb': nrm(k[18], (N_ODD, D_MODEL, D_MODEL), D_MODEL ** -0.5),
        'w_gate': nrm(k[19], (DEPTH, D_MODEL, D_FF), D_MODEL ** -0.5),
        'w_up': nrm(k[20], (DEPTH, D_MODEL, D_FF), D_MODEL ** -0.5),
        'w_down': nrm(k[21], (DEPTH, D_FF, D_MODEL), D_FF ** -0.5),
    }


def reference(x_prompt, x_sample, state_pool, cache_k, cache_v, norm_mix_pre, norm_mix_post,
              norm_ffn_pre, norm_ffn_post, w_in_ab, ln_v_g, ln_v_b, w_spatial, b_spatial,
              w_pool_map, pool_scale, w_out_ab, w_qkv, w_o_sb, w_gate, w_up, w_down):
    p = dict(norm_mix_pre=norm_mix_pre, norm_mix_post=norm_mix_post, norm_ffn_pre=norm_ffn_pre,
             norm_ffn_post=norm_ffn_post, w_in_ab=w_in_ab, ln_v_g=ln_v_g, ln_v_b=ln_v_b,
             w_spatial=w_spatial, b_spatial=b_spatial, w_pool_map=w_pool_map, pool_scale=pool_scale,
             w_out_ab=w_out_ab, w_qkv=w_qkv, w_o_sb=w_o_sb, w_gate=w_gate, w_up=w_up, w_down=w_down)
    y_prompt, pool_p, k_p, v_p, _ = trunk(x_prompt, True, state_pool, cache_k, cache_v, p)
    y_sample, pool_s, k_s, v_s, vn_s = trunk(x_sample, False, state_pool, cache_k, cache_v, p)
    return (y_prompt, y_sample, pool_p, k_p, v_p, pool_s, k_s, v_s, vn_s)
```

```python
import numpy as np
from contextlib import ExitStack
import concourse.bass as bass
import concourse.mybir as mybir
from concourse.bass_utils import run_bass_kernel_spmd

F32 = mybir.dt.float32
BF16 = mybir.dt.bfloat16
I32 = mybir.dt.int32
AF = mybir.ActivationFunctionType
ALU = mybir.AluOpType

D = 1024
S = 2048
NTT = 16
DS = 16
DFF = 2816
NJ = 22
EPS = 1e-6
POOL_W = (2, 4, 8, 16)
NR_STEPS = 2

ENGS = ("pe", "act", "dve", "pool", "sp")
NDSEM = 12
PG = 512


class Buf:
    def __init__(self, ap, keys, name=""):
        self.ap = ap
        self.keys = tuple(keys)
        self.name = name

    def sub(self, ap, keys=None):
        return Buf(ap, self.keys if keys is None else keys, self.name)


def _keys(items):
    out = []
    for it in items:
        if it is None:
            continue
        if isinstance(it, Buf):
            out.extend(it.keys)
        elif isinstance(it, (list, tuple)) and it and isinstance(it[0], (Buf, list)):
            out.extend(_keys(it))
        else:
            out.append(it)
    return out


class Prog:
    def __init__(self):
        self.ops = []

    def add(self, eng, fn, reads=(), writes=(), dma=False):
        self.ops.append(dict(eng=eng, fn=fn, reads=tuple(dict.fromkeys(_keys(reads))),
                             writes=tuple(dict.fromkeys(_keys(writes))), dma=dma))

    def resolve(self):
        last_w = {}
        readers = {}
        ps_last = {}
        ops = self.ops
        for i, op in enumerate(ops):
            deps = set()
            e = op["eng"]
            for r in op["reads"]:
                if r in last_w:
                    deps.add(last_w[r])
            for w in op["writes"]:
                if w in last_w:
                    deps.add(last_w[w])
                for rd in readers.get(w, ()):
                    deps.add(rd)
            for k in op["reads"] + op["writes"]:
                if isinstance(k, tuple) and k and k[0] == "ps":
                    la = ps_last.setdefault(k, {})
                    for e2, j in la.items():
                        if e2 != e:
                            deps.add(j)
                    la[e] = i
            deps.discard(i)
            keep = set()
            for d in deps:
                dop = ops[d]
                if dop["eng"] == e and e == "pe" and not dop["dma"] and not op["dma"]:
                    continue
                keep.add(d)
            op["deps"] = keep
            for w in op["writes"]:
                last_w[w] = i
                readers[w] = []
            for r in op["reads"]:
                if r not in op["writes"]:
                    readers.setdefault(r, []).append(i)
        needed = set()
        for op in ops:
            needed |= op["deps"]
        self.needed = needed

    def emit(self, block, sems, dsems):
        self.resolve()
        ops = self.ops
        cnt = {e: 0 for e in ENGS}
        dcnt = {e: 0 for e in ENGS}
        dval = {e: [0] * NDSEM for e in ENGS}
        for i, op in enumerate(ops):
            e = op["eng"]
            if op["dma"]:
                k = dcnt[e] % NDSEM
                dcnt[e] += 1
                op["prev_tok"] = (("d", e, k), dval[e][k])
                dval[e][k] += 16
                op["tok"] = (("d", e, k), dval[e][k])
            elif i in self.needed:
                cnt[e] += 1
                op["tok"] = (("c", e), cnt[e])
            else:
                op["tok"] = None
        per_eng = {e: [] for e in ENGS}
        for i, op in enumerate(ops):
            per_eng[op["eng"]].append(i)

        def semof(key):
            return sems[key[1]] if key[0] == "c" else dsems[key[1]][key[2]]

        final_tokens = {}
        for op in ops:
            if op["dma"]:
                final_tokens[op["tok"][0]] = op["tok"][1]

        def make_stream(e):
            def stream(eng):
                waited = {}
                for i in per_eng[e]:
                    op = ops[i]
                    want = {}
                    for d in op["deps"]:
                        key, val = ops[d]["tok"]
                        if want.get(key, 0) < val:
                            want[key] = val
                    if op["dma"]:
                        key, val = op["prev_tok"]
                        if val > 0 and want.get(key, 0) < val:
                            want[key] = val
                    for key, val in want.items():
                        if waited.get(key, 0) >= val:
                            continue
                        eng.wait_ge(semof(key), val)
                        waited[key] = val
                    ins = op["fn"](eng)
                    if op["tok"] is not None:
                        ins.then_inc(semof(op["tok"][0]), 16 if op["dma"] else 1)
                if e == "sp":
                    for key, val in final_tokens.items():
                        if waited.get(key, 0) < val:
                            eng.wait_ge(semof(key), val)
            return stream

        block.tensor(make_stream("pe"))
        block.scalar(make_stream("act"))
        block.vector(make_stream("dve"))
        block.gpsimd(make_stream("pool"))
        block.sync(make_stream("sp"))


ARENA_BYTES = 210944
PERSIST_BYTES = 73216


class Builder:
    def __init__(self, stop_after=None):
        self.stop_after = stop_after
        self.nc = bass.Bass("TRN2", target_bir_lowering=False)
        self.P = Prog()
        self.es = ExitStack()
        self.dram = {}
        self.bank_rr = 0
        self.bank_pool = list(range(8))
        self.uid = 0

    def din(self, name, shape, dt=F32):
        t = self.nc.dram_tensor(name, list(shape), dt, kind="ExternalInput").ap()
        self.dram[name] = t
        return t

    def dout(self, name, shape, dt=F32):
        t = self.nc.dram_tensor(name, list(shape), dt, kind="ExternalOutput").ap()
        self.dram[name] = t
        return t

    def view(self, off, shape, dt, name=""):
        esz = 4 if dt in (F32, I32) else 2
        free = 1
        for s in shape[1:]:
            free *= s
        nbytes = free * esz
        assert off % 4 == 0 and off + nbytes <= ARENA_BYTES, (name, off, nbytes)
        ap = self.arena[0:shape[0], off // 2:(off + nbytes) // 2]
        if esz == 4:
            ap = ap.bitcast(dt)
        if len(shape) == 3:
            ap = ap.rearrange("p (a b) -> p a b", a=shape[1])
        elif len(shape) == 4:
            ap = ap.rearrange("p (a b c) -> p a b c", a=shape[1], b=shape[2])
        keys = [("pg", i) for i in range(off // PG, (off + nbytes + PG - 1) // PG)]
        return Buf(ap, keys, name)

    def small(self, name, shape, dt=F32):
        t = self.es.enter_context(self.nc.sbuf_tensor(name, list(shape), dt))
        return Buf(t[:], [("sm", name)], name)

    def bank(self):
        b = self.bank_pool[self.bank_rr % len(self.bank_pool)]
        self.bank_rr += 1
        return b

    def psk(self, b):
        return ("ps", b)

    def mm(self, out, lhsT, rhs, start, stop, reads, writes, skip=False):
        if skip:
            fn = lambda t: t.matmul(out, lhsT=lhsT, rhs=rhs, start=start, stop=stop, skip_group_check=True)
        else:
            fn = lambda t: t.matmul(out, lhsT=lhsT, rhs=rhs, start=start, stop=stop)
        self.P.add("pe", fn, reads, writes)

    def tr(self, out, in_, ident, reads, writes):
        self.P.add("pe", lambda t: t.transpose(out=out, in_=in_, identity=ident), reads, writes)

    def act(self, out, in_, func, reads, writes, bias=None, scale=None, accum_out=None):
        kw = {}
        if bias is not None:
            kw["bias"] = bias
        if scale is not None:
            kw["scale"] = scale
        if accum_out is not None:
            kw["accum_out"] = accum_out
        self.P.add("act", lambda a: a.activation(out=out, in_=in_, func=func, **kw), reads, writes)

    def tt(self, eng, out, in0, in1, op, reads, writes):
        self.P.add(eng, lambda v: v.tensor_tensor(out=out, in0=in0, in1=in1, op=op), reads, writes)

    def ts(self, out, in0, s1, s2, op0, op1, reads, writes):
        if op1 is None:
            self.P.add("dve", lambda v: v.tensor_scalar(out=out, in0=in0, scalar1=s1, scalar2=None, op0=op0), reads, writes)
        else:
            self.P.add("dve", lambda v: v.tensor_scalar(out=out, in0=in0, scalar1=s1, scalar2=s2, op0=op0, op1=op1), reads, writes)

    def stt(self, out, in0, scalar, in1, op0, op1, reads, writes):
        self.P.add("dve", lambda v: v.scalar_tensor_tensor(out=out, in0=in0, scalar=scalar, in1=in1, op0=op0, op1=op1), reads, writes)

    def cp(self, eng, out, in_, reads, writes):
        if eng == "act":
            self.P.add("act", lambda a: a.copy(out=out, in_=in_), reads, writes)
        else:
            self.P.add(eng, lambda v: v.tensor_copy(out=out, in_=in_), reads, writes)

    def memset(self, eng, ap, val, writes):
        self.P.add(eng, lambda v: v.memset(ap, val), (), writes)

    def dma(self, q, out, in_, reads, writes, slow=False):
        if slow:
            self.P.add(q, lambda e: e.dma_start(out=out, in_=in_, allow_slow_non_contiguous=True), reads, writes, dma=True)
        else:
            self.P.add(q, lambda e: e.dma_start(out=out, in_=in_), reads, writes, dma=True)

    def rsqrt(self, st, n, m, src, add_eps=True):
        a = st["a"].ap[:n, 0:m]
        y = st["y"].ap[:n, 0:m]
        t1 = st["t1"].ap[:n, 0:m]
        t2 = st["t2"].ap[:n, 0:m]
        A, Y, T1, T2 = st["a"], st["y"], st["t1"], st["t2"]
        self.ts(a, src[0], EPS, None, ALU.add, None, [src[1]], [A])
        self.ts(y.bitcast(I32), a.bitcast(I32), 1, None, ALU.arith_shift_right, None, [A], [Y])
        self.ts(y.bitcast(I32), y.bitcast(I32), -1, 0x5F3759DF, ALU.mult, ALU.add, [Y], [Y])
        for _ in range(NR_STEPS):
            self.tt("dve", t1, a, y, ALU.mult, [A, Y], [T1])
            self.stt(t2, t1, -0.5, y, ALU.mult, ALU.mult, [T1, Y], [T2])
            self.stt(y, t2, 1.5, y, ALU.add, ALU.mult, [T2, Y], [Y])

    def statset(self, name, m=8):
        if not hasattr(self, "_stat_t"):
            self._stat_t = self.es.enter_context(self.nc.sbuf_tensor("stats", [128, 448], F32))
            self._stat_off = 0
            self._stat_sets = {}
        if name in self._stat_sets:
            return self._stat_sets[name]
        out = {}
        for k in ("ss", "s2", "a", "y", "t1", "t2", "mu"):
            o = self._stat_off
            self._stat_off += m
            assert self._stat_off <= 448
            out[k] = Buf(self._stat_t[:, o:o + m], [("sm", f"{name}_{k}")], f"{name}_{k}")
        self._stat_sets[name] = out
        return out

    def build(self):
        nc, es = self.nc, self.es
        din, dout = self.din, self.dout
        xp = din("xp", [S, D]); xs = din("xs", [DS, D]); spool = din("spool", [15, 512])
        ck = din("ck", [S, D]); cv = din("cv", [S, D])
        nmp = din("nmp", [2, D]); nmq = din("nmq", [2, D]); nfp = din("nfp", [2, D]); nfq = din("nfq", [2, D])
        w_in = din("w_in", [D, 1536]); ln_g = din("ln_g", [512]); ln_b = din("ln_b", [512])
        w_sp = din("w_sp", [4, 128, 128]); b_sp = din("b_sp", [4, 128]); w_map = din("w_map", [4, 128, 128])
        p_scale = din("p_scale", [512]); w_out = din("w_out", [D, D])
        w_qkv = din("w_qkv", [D, 3 * D]); w_o = din("w_o", [D, D])
        w_gate = din("w_gate", [2, D, DFF]); w_up = din("w_up", [2, D, DFF]); w_down = din("w_down", [2, DFF, D])
        c_ident = din("c_ident", [128, 128]); c_negT = din("c_negT", [128, 128]); c_negones = din("c_negones", [128, 128])
        c_tri = din("c_tri", [128, 128]); c_cmask = din("c_cmask", [128, 128]); c_invc = din("c_invc", [64])
        c_onesrow = din("c_onesrow", [1, 128])
        y_p = dout("y_p", [S, D]); y_s = dout("y_s", [DS, D]); pool_p = dout("pool_p", [15, 512])
        k_p = dout("k_p", [S, D]); v_p = dout("v_p", [S, D]); pool_s = dout("pool_s", [15, 512])
        k_s = dout("k_s", [DS, D]); v_s = dout("v_s", [DS, D]); vn_s = dout("vn_s", [DS, 512])

        arena_t = es.enter_context(nc.sbuf_tensor("arena", [128, ARENA_BYTES // 2], BF16))
        self.arena = arena_t
        PS = [es.enter_context(nc.psum_tensor(f"psb{i}", [128, 512], F32)) for i in range(8)]
        self.PS = PS
        sems = {e: es.enter_context(nc.semaphore("s_" + e)) for e in ENGS}
        dsems = {e: [es.enter_context(nc.semaphore(f"d_{e}{k}")) for k in range(NDSEM)] for e in ("sp", "pool")}
        view = self.view
        P = self.P
        psk = self.psk

        off = 0
        Xall = view(off, [128, NTT, D], F32, "X"); off += 65536
        X = [Xall.sub(Xall.ap[:, t, :], Xall.keys[t * 8:(t + 1) * 8]) for t in range(NTT)]
        XS = view(off, [128, D], F32, "XS"); off += 4096
        identb = view(off, [128, 128], BF16, "identb"); off += 512
        identf = view(off, [128, 128], F32, "identf"); off += 512
        negT = view(off, [128, 128], BF16, "negT"); off += 512
        negones = view(off, [128, 128], BF16, "negones"); off += 512
        tri = view(off, [128, 128], BF16, "tri"); off += 512
        onesrow = view(off, [128, 128], BF16, "onesrow"); off += 512
        gcol = view(off, [128, 4, 8], F32, "gcol"); off += 512
        assert off <= PERSIST_BYTES
        BASE = PERSIST_BYTES

        self.dma("pool", identb.ap, c_ident[:, :], [], [identb])
        self.dma("sp", identf.ap, c_ident[:, :], [], [identf])
        self.dma("pool", negT.ap, c_negT[:, :], [], [negT])
        self.dma("pool", negones.ap, c_negones[:, :], [], [negones])
        self.dma("pool", tri.ap, c_tri[:, :], [], [tri])
        self.dma("pool", onesrow.ap[0:1, :], c_onesrow[:, :], [], [onesrow])
        for wi, src in enumerate((nmp[0], nfp[0], nmp[1], nfp[1])):
            self.dma("sp", gcol.ap[:, wi, :], src.rearrange("(k p) -> p k", p=128), [], [gcol], slow=True)
        for t in range(4):
            self.dma("sp", X[t].ap, xp[t * 128:(t + 1) * 128, :], [], [X[t]])
        self.dma("sp", XS.ap[0:DS, :], xs[:, :], [], [XS])

        self.groups = [dict(name=f"tg{g}", tiles=[(X[4 * g + i], 128, 4 * g + i) for i in range(4)], NT=512, sample=False)
                       for g in range(4)]
        self.sgroup = dict(name="smp", tiles=[(XS, DS, None)], NT=DS, sample=True)
        self.consts = dict(identb=identb, identf=identf, negT=negT, negones=negones, tri=tri, onesrow=onesrow, gcol=gcol)
        self.io = locals()

        self.phase_A(BASE)
        if self.stop_after == "A":
            return self.finish(sems, dsems)
        self.phase_ffn(BASE, 0)
        if self.stop_after == "B":
            return self.finish(sems, dsems)
        self.phase_C(BASE)
        if self.stop_after == "C":
            return self.finish(sems, dsems)
        self.phase_ffn(BASE, 1)
        return self.finish(sems, dsems)

    def finish(self, sems, dsems):
        io = self.io
        X, XS = io["X"], io["XS"]
        for t in range(NTT):
            self.dma("sp", io["y_p"][t * 128:(t + 1) * 128, :], X[t].ap, [X[t]], [])
        self.dma("sp", io["y_s"][:, :], XS.ap[0:DS, :], [XS], [])
        block = self.es.enter_context(self.nc.Block())
        self.P.emit(block, sems, dsems)
        self.es.close()
        return self.nc

    def prenorm(self, tiles, which, hT, st, Hbs, col0=0, split=False):
        c = self.consts
        m = len(tiles)
        n0 = tiles[0][1]
        if split:
            self.memset("dve", st["ss"].ap[:n0, 0:m], 0.0, [st["ss"]])
        for i, (xb, n, _) in enumerate(tiles):
            hb = Hbs[i % len(Hbs)]
            if split and i % 2 == 1:
                self.P.add("dve", lambda v, o=hb.ap[:n, :], x=xb.ap[:n, :], a=st["ss"].ap[:n, i:i + 1]: v.scalar_tensor_tensor(
                    out=o, in0=x, scalar=1.0 / 1024.0, in1=x, op0=ALU.mult, op1=ALU.mult, accum_out=a), [xb, st["ss"]], [hb, st["ss"]])
            else:
                self.act(hb.ap[:n, :], xb.ap[:n, :], AF.Square, [xb], [hb, st["ss"]], scale=1.0 / 32.0,
                         accum_out=st["ss"].ap[:n, i:i + 1])
        self.rsqrt(st, n0, m, (st["ss"].ap[:n0, 0:m], st["ss"]))
        for i, (xb, n, _) in enumerate(tiles):
            hb = Hbs[i % len(Hbs)]
            if split and i % 2 == 1:
                self.ts(hb.ap[:n, :], xb.ap[:n, :], st["y"].ap[:n, i:i + 1], None, ALU.mult, None, [xb, st["y"]], [hb])
            else:
                self.act(hb.ap[:n, :], xb.ap[:n, :], AF.Copy, [xb, st["y"]], [hb], scale=st["y"].ap[:n, i:i + 1])
            b = self.bank()
            pv = self.PS[b][:].bitcast(BF16).rearrange("p (k t) -> p k t", k=8)
            for k in range(8):
                self.tr(pv[:, k, 0:n], hb.ap[:n, k * 128:(k + 1) * 128], c["identb"].ap[:n, :n], [hb, c["identb"]], [self.psk(b)])
            c0 = col0 + i * 128
            g = c["gcol"].ap[:, which, :]
            self.tt("dve", hT.ap[:, :, c0:c0 + n], pv[:, :, 0:n], g.unsqueeze(2).to_broadcast([128, 8, n]), ALU.mult,
                    [self.psk(b), c["gcol"]], [hT])

    def prenorm_front(self, tiles, st, Hbs):
        assert len(tiles) <= len(Hbs)
        m = len(tiles)
        n0 = tiles[0][1]
        for i, (xb, n, _) in enumerate(tiles):
            self.act(Hbs[i].ap[:n, :], xb.ap[:n, :], AF.Square, [xb], [Hbs[i], st["ss"]], scale=1.0 / 32.0,
                     accum_out=st["ss"].ap[:n, i:i + 1])
        self.rsqrt(st, n0, m, (st["ss"].ap[:n0, 0:m], st["ss"]))
        for i, (xb, n, _) in enumerate(tiles):
            self.act(Hbs[i].ap[:n, :], xb.ap[:n, :], AF.Copy, [xb, st["y"]], [Hbs[i]], scale=st["y"].ap[:n, i:i + 1])

    def prenorm_back(self, tiles, which, hT, Hbs, cols):
        c = self.consts
        for i, (xb, n, _) in enumerate(tiles):
            hb = Hbs[i]
            b = self.bank()
            pv = self.PS[b][:].bitcast(BF16).rearrange("p (k t) -> p k t", k=8)
            for k in range(8):
                self.tr(pv[:, k, 0:n], hb.ap[:n, k * 128:(k + 1) * 128], c["identb"].ap[:n, :n], [hb, c["identb"]], [self.psk(b)])
            g = c["gcol"].ap[:, which, :]
            self.tt("dve", hT.ap[:, :, cols[i]:cols[i] + n], pv[:, :, 0:n], g.unsqueeze(2).to_broadcast([128, 8, n]), ALU.mult,
                    [self.psk(b), c["gcol"]], [hT])

    def postnorm(self, xb, n, banks, Gpost, st, tmps, add_eng):
        PS = self.PS
        for h, b in enumerate(banks):
            tmp = tmps[h]
            self.act(tmp.ap[:n, :], PS[b][:n, :], AF.Square, [self.psk(b)], [tmp, st["ss"]], scale=1.0 / 32.0,
                     accum_out=st["ss"].ap[:n, h:h + 1])
        self.tt("dve", st["s2"].ap[:n, 0:1], st["ss"].ap[:n, 0:1], st["ss"].ap[:n, 1:2], ALU.add, [st["ss"]], [st["s2"]])
        self.rsqrt(st, n, 1, (st["s2"].ap[:n, 0:1], st["s2"]))
        for h, b in enumerate(banks):
            tmp = tmps[h]
            self.stt(tmp.ap[:n, :], PS[b][:n, :], st["y"].ap[:n, 0:1], Gpost.ap[:n, h * 512:(h + 1) * 512], ALU.mult, ALU.mult,
                     [self.psk(b), st["y"], Gpost], [tmp])
            self.tt(add_eng, xb.ap[:n, h * 512:(h + 1) * 512], xb.ap[:n, h * 512:(h + 1) * 512], tmp.ap[:n, :], ALU.add,
                    [xb, tmp], [xb])

    def postnorm_multi(self, items, Gpost, st, tmps, add_eng):
        PS = self.PS
        m = len(items)
        n0 = items[0][1]
        for t, (xb, n, banks) in enumerate(items):
            for h, b in enumerate(banks):
                tmp = tmps[2 * t + h]
                self.act(tmp.ap[:n, :], PS[b][:n, :], AF.Square, [self.psk(b)], [tmp, st["ss"]], scale=1.0 / 32.0,
                         accum_out=st["ss"].ap[:n, 2 * t + h:2 * t + h + 1])
        self.tt("dve", st["s2"].ap[:n0, 0:m], st["ss"].ap[:n0, 0:2 * m:2], st["ss"].ap[:n0, 1:2 * m:2], ALU.add, [st["ss"]], [st["s2"]])
        self.rsqrt(st, n0, m, (st["s2"].ap[:n0, 0:m], st["s2"]))
        for t, (xb, n, banks) in enumerate(items):
            for h, b in enumerate(banks):
                tmp = tmps[2 * t + h]
                self.stt(tmp.ap[:n, :], PS[b][:n, :], st["y"].ap[:n, t:t + 1], Gpost.ap[:n, h * 512:(h + 1) * 512], ALU.mult, ALU.mult,
                         [self.psk(b), st["y"], Gpost], [tmp])
                self.tt(add_eng, xb.ap[:n, h * 512:(h + 1) * 512], xb.ap[:n, h * 512:(h + 1) * 512], tmp.ap[:n, :], ALU.add,
                        [xb, tmp], [xb])

    def load_w8(self, dst, src2d, ncols):
        for k in range(8):
            self.dma("pool", dst.ap[:, k, :], src2d[k * 128:(k + 1) * 128, :], [], [dst])

    def load_bcast(self, dst, vec, n):
        self.dma("sp", dst.ap[:, 0:n], vec.partition_broadcast(128), [], [dst])

    def phase_A(self, BASE):
        io, c, view, PS, psk = self.io, self.consts, self.view, self.PS, self.psk
        nc = self.nc
        off = [BASE]

        def take(shape, dt, name):
            esz = 4 if dt in (F32, I32) else 2
            free = 1
            for s in shape[1:]:
                free *= s
            b = view(off[0], shape, dt, name)
            off[0] += (free * esz + PG - 1) // PG * PG
            return b

        Win = take([128, 8, 1536], BF16, "Win")
        Wout = take([128, 8, 1024], BF16, "Wout")
        hTs = [take([128, 8, 512], BF16, f"hT{i}") for i in range(2)]
        uTs = [take([128, 4, 512], BF16, f"uT{i}") for i in range(2)]
        pTs = [take([128, 4, 528], F32, f"pT{i}") for i in range(2)]
        vfs = [take([128, 512], F32, f"vf{i}") for i in range(4)]
        vn16s = [take([128, 512], BF16, f"vn16{i}") for i in range(4)]
        vnf = vfs[1]
        abT = take([128, 8, 512], BF16, "abT")
        dT = take([128, 4, 512], BF16, "dT")
        plA = take([128, 528], F32, "plA")
        plB = take([128, 528], F32, "plB")
        Hbs = [take([128, 1024], BF16, f"Hb{i}") for i in range(2)]
        tmps = [take([128, 512], F32, f"tmp{i}") for i in range(2)]
        tmps4 = tmps + [plA.sub(plA.ap[:, 0:512]), plB.sub(plB.ap[:, 0:512])]
        Gpost = take([128, 1024], F32, "Gpost")
        Gln = take([128, 512], F32, "Gln")
        Bln = take([128, 512], F32, "Bln")
        WsT = take([128, 4, 128], BF16, "WsT")
        Wmap = take([128, 4, 128], BF16, "Wmap")
        bsp = take([128, 4, 128], BF16, "bsp")
        wsf = take([128, 128], F32, "wsf")
        cmask = take([128, 128], F32, "cmask")
        invc = take([128, 4, 16], F32, "invc")
        pscale = take([128, 4], F32, "pscale")
        spst = take([128, 512], F32, "spst")
        tailst = spst
        halo_s = take([128, 4, 16], F32, "halo_s")
        t16 = take([128, 16], F32, "t16")
        assert off[0] <= ARENA_BYTES, off[0]
        st_pre = self.statset("pre")
        st_ln = self.statset("ln")
        st_post = self.statset("post")

        self.load_w8(Win, io["w_in"], 1536)
        Win_u = Win_v = Win_p = Win
        self.load_w8(Wout, io["w_out"], 1024)
        self.load_bcast(Gpost, io["nmq"][0], 1024)
        self.load_bcast(Gln, io["ln_g"], 512)
        self.load_bcast(Bln, io["ln_b"], 512)
        self.dma("sp", cmask.ap, io["c_cmask"][:, :], [], [cmask])
        self.dma("sp", invc.ap.rearrange("p g t -> p (g t)"), io["c_invc"].partition_broadcast(128), [], [invc])
        for g in range(4):
            self.dma("pool", Wmap.ap[:, g, :], io["w_map"][g], [], [Wmap])
            self.dma("pool", bsp.ap[0:1, g, :], io["b_sp"][g:g + 1, :], [], [bsp])
            self.dma("sp", pscale.ap[:, g:g + 1], io["p_scale"][g * 128:(g + 1) * 128].rearrange("(p o) -> p o", o=1), [], [pscale], slow=True)
            self.dma("sp", wsf.ap, io["w_sp"][g], [], [wsf])
            b = self.bank()
            self.tr(PS[b][:, 0:128], wsf.ap, c["identf"].ap, [wsf, c["identf"]], [psk(b)])
            self.tt("dve", WsT.ap[:, g, :], PS[b][:, 0:128], cmask.ap, ALU.mult, [psk(b), cmask], [WsT])
        self.dma("sp", spst.ap[0:15, :], io["spool"][:, :], [], [spst])
        b = self.bank()
        for cc in range(4):
            self.tr(PS[b][:, cc * 16:cc * 16 + 15], spst.ap[0:15, cc * 128:(cc + 1) * 128], c["identf"].ap[:15, :15],
                    [spst, c["identf"]], [psk(b)])
        self.cp("act", halo_s.ap[:, :, 0:15], PS[b][:, 0:64].rearrange("p (c t) -> p c t", c=4)[:, :, 0:15], [psk(b)], [halo_s])

        for t in range(4, NTT):
            self.dma("sp", io["X"][t].ap, io["xp"][t * 128:(t + 1) * 128, :], [], [io["X"][t]])
        groups = self.groups + [self.sgroup]

        def stX(gi):
            grp = groups[gi]
            hT, pT, NT, smp = hTs[gi % 2], pTs[gi % 2], grp["NT"], grp["sample"]
            uT = uTs[gi % 2]
            if gi == 0:
                self.memset("pool", pT.ap[:, :, 0:15], 0.0, [pT])
            elif smp:
                self.cp("pool", pT.ap[:, :, 0:15], halo_s.ap[:, :, 0:15], [halo_s], [pT])
            else:
                pprev = pTs[(gi - 1) % 2]
                self.cp("pool", pT.ap[:, :, 0:15], pprev.ap[:, :, 512:527], [pprev], [pT])
            for cc in range(4):
                b = self.bank()
                for k in range(8):
                    self.mm(PS[b][:, 0:NT], Win.ap[:, k, cc * 128:(cc + 1) * 128], hT.ap[:, k, 0:NT], k == 0, k == 7, [Win_u, hT], [psk(b)])
                self.act(uT.ap[:, cc, 0:NT], PS[b][:, 0:NT], AF.Gelu, [psk(b)], [uT])
            for cc in range(4):
                b = self.bank()
                for k in range(8):
                    self.mm(PS[b][:, 0:NT], Win.ap[:, k, 1024 + cc * 128:1024 + (cc + 1) * 128], hT.ap[:, k, 0:NT], k == 0, k == 7,
                            [Win_p, hT], [psk(b)])
                self.cp("act", pT.ap[:, cc, 15:15 + NT], PS[b][:, 0:NT], [psk(b)], [pT])

        def stYv(gi):
            grp = groups[gi]
            hT, tiles, smp = hTs[gi % 2], grp["tiles"], grp["sample"]
            sl = st_ln
            for i, (xb, n, tt) in enumerate(tiles):
                b = self.bank()
                for k in range(8):
                    self.mm(PS[b][:n, :], hT.ap[:, k, i * 128:i * 128 + n], Win.ap[:, k, 512:1024], k == 0, k == 7, [Win_v, hT], [psk(b)])
                self.act(vfs[i].ap[:n, :], PS[b][:n, :], AF.Gelu, [psk(b)], [vfs[i], sl["ss"]], accum_out=sl["ss"].ap[:n, i:i + 1])
                self.act(vn16s[i].ap[:n, :], vfs[i].ap[:n, :], AF.Square, [vfs[i]], [vn16s[i], sl["s2"]], accum_out=sl["s2"].ap[:n, i:i + 1])

        def stY(gi):
            grp = groups[gi]
            hT, tiles, smp = hTs[gi % 2], grp["tiles"], grp["sample"]
            uT = uTs[gi % 2]
            m = len(tiles)
            n0 = tiles[0][1]
            sl = st_ln
            self.ts(sl["mu"].ap[:n0, 0:m], sl["ss"].ap[:n0, 0:m], 1.0 / 512.0, None, ALU.mult, None, [sl["ss"]], [sl["mu"]])
            self.tt("dve", sl["t1"].ap[:n0, 0:m], sl["mu"].ap[:n0, 0:m], sl["mu"].ap[:n0, 0:m], ALU.mult, [sl["mu"]], [sl["t1"]])
            self.stt(sl["t2"].ap[:n0, 0:m], sl["s2"].ap[:n0, 0:m], 1.0 / 512.0, sl["t1"].ap[:n0, 0:m], ALU.mult, ALU.subtract,
                     [sl["s2"], sl["t1"]], [sl["t2"]])
            self.rsqrt(sl, n0, m, (sl["t2"].ap[:n0, 0:m], sl["t2"]))
            for i, (xb, n, tt) in enumerate(tiles):
                vf, vn16 = vfs[i], vn16s[i]
                self.stt(vf.ap[:n, :], vf.ap[:n, :], sl["mu"].ap[:n, i:i + 1], Gln.ap[:n, :], ALU.subtract, ALU.mult, [vf, sl["mu"], Gln], [vf])
                if smp:
                    self.stt(vnf.ap[:n, :], vf.ap[:n, :], sl["y"].ap[:n, i:i + 1], Bln.ap[:n, :], ALU.mult, ALU.add, [vf, sl["y"], Bln], [vnf])
                    self.dma("sp", io["vn_s"][:, :], vnf.ap[:n, :], [vnf], [])
                    self.cp("dve", vn16.ap[:n, :], vnf.ap[:n, :], [vnf], [vn16])
                else:
                    self.stt(vn16.ap[:n, :], vf.ap[:n, :], sl["y"].ap[:n, i:i + 1], Bln.ap[:n, :], ALU.mult, ALU.add, [vf, sl["y"], Bln], [vn16])

        def stYg(gi):
            grp = groups[gi]
            tiles = grp["tiles"]
            uT = uTs[gi % 2]
            for i, (xb, n, tt) in enumerate(tiles):
                vn16 = vn16s[i]
                b2 = self.bank()
                pv = PS[b2][:].rearrange("p (g i) -> p g i", g=4)
                for g in range(4):
                    self.mm(pv[:, g, 0:n], vn16.ap[:n, g * 128:(g + 1) * 128], WsT.ap[:n, g, 0:n], True, False, [vn16, WsT], [psk(b2)], skip=True)
                    self.mm(pv[:, g, 0:n], c["onesrow"].ap[0:1, :], bsp.ap[0:1, g, 0:n], False, True, [c["onesrow"], bsp], [psk(b2)], skip=True)
                c0 = i * 128
                self.tt("dve", abT.ap[:, 0:4, c0:c0 + n], uT.ap[:, :, c0:c0 + n], pv[:, :, 0:n], ALU.mult, [uT, psk(b2)], [abT])

        def stZ(gi):
            grp = groups[gi]
            hT, pT, NT, tiles, smp = hTs[gi % 2], pTs[gi % 2], grp["NT"], grp["tiles"], grp["sample"]
            Wd_ = 15 + NT
            for g in range(4):
                src = pT.ap[:, g, :]
                cur = src
                curb = pT
                bufs = [plA, plB]
                s_ = 1
                for stage in range(g + 1):
                    dst = bufs[stage % 2]
                    lo = 2 * s_ - 1
                    self.tt("pool", dst.ap[:, lo:Wd_], cur[:, lo:Wd_], cur[:, lo - s_:Wd_ - s_], ALU.add, [curb], [dst])
                    cur = dst.ap
                    curb = dst
                    s_ *= 2
                w = POOL_W[g]
                self.stt(dT.ap[:, g, 0:NT], cur[:, 15:15 + NT], 1.0 / w, src[:, 15:15 + NT], ALU.mult, ALU.subtract, [curb, pT], [dT])
                if gi == 0:
                    self.tt("dve", t16.ap[:, :], cur[:, 15:31], invc.ap[:, g, :], ALU.mult, [curb, invc], [t16])
                    self.tt("dve", dT.ap[:, g, 0:16], t16.ap[:, :], src[:, 15:31], ALU.subtract, [t16, pT], [dT])

        def stZm(gi):
            grp = groups[gi]
            hT, pT, NT, tiles, smp = hTs[gi % 2], pTs[gi % 2], grp["NT"], grp["tiles"], grp["sample"]
            for g in range(4):
                b = self.bank()
                self.mm(PS[b][:, 0:NT], Wmap.ap[:, g, :], dT.ap[:, g, 0:NT], True, True, [Wmap, dT], [psk(b)])
                self.act(abT.ap[:, 4 + g, 0:NT], PS[b][:, 0:NT], AF.Copy, [psk(b), pscale], [abT], scale=pscale.ap[:, g:g + 1])
            if smp or gi == 3:
                i_last = len(tiles) - 1
                n = tiles[i_last][1]
                b = self.bank()
                for k in range(8):
                    self.mm(PS[b][:n, :], hT.ap[:, k, i_last * 128:i_last * 128 + n], Win.ap[:, k, 1024:1536], k == 0, k == 7, [Win_p, hT], [psk(b)])
                self.cp("act", tailst.ap[:n, :], PS[b][:n, :], [psk(b)], [tailst])
                if smp:
                    self.dma("sp", io["pool_s"][:, :], tailst.ap[1:16, :], [tailst], [])
                else:
                    self.dma("sp", io["pool_p"][:, :], tailst.ap[113:128, :], [tailst], [])

        def stW(gi):
            tiles = groups[gi]["tiles"]
            for p0 in range(0, len(tiles), 2):
                items = []
                for i in range(p0, min(p0 + 2, len(tiles))):
                    xb, n, tt = tiles[i]
                    banks = [self.bank(), self.bank()]
                    for h, b in enumerate(banks):
                        for k in range(8):
                            self.mm(PS[b][:n, :], abT.ap[:, k, i * 128:i * 128 + n], Wout.ap[:, k, h * 512:(h + 1) * 512], k == 0, k == 7,
                                    [abT, Wout], [psk(b)])
                    items.append((xb, n, banks))
                self.postnorm_multi(items, Gpost, st_post, tmps4, "pool")

        ng = len(groups)
        self.prenorm(groups[0]["tiles"], 0, hTs[0], st_pre, Hbs)
        stX(0)
        stZ(0)
        for gi in range(ng):
            nxt = groups[gi + 1]["tiles"] if gi + 1 < ng else None
            hTn = hTs[(gi + 1) % 2]
            if nxt is not None:
                self.prenorm_front(nxt[0:2], st_pre, Hbs)
            stYv(gi)
            if nxt is not None:
                self.prenorm_back(nxt[0:2], 0, hTn, Hbs, [0, 128])
                if len(nxt) > 2:
                    self.prenorm_front(nxt[2:4], st_pre, Hbs)
            if gi >= 1:
                stW(gi - 1)
            if nxt is not None and len(nxt) > 2:
                self.prenorm_back(nxt[2:4], 0, hTn, Hbs, [256, 384])
            stY(gi)
            if nxt is not None:
                stX(gi + 1)
            stYg(gi)
            stZm(gi)
            if nxt is not None:
                stZ(gi + 1)
        stW(ng - 1)

    def phase_ffn(self, BASE, layer):
        io, c, view, PS, psk = self.io, self.consts, self.view, self.PS, self.psk
        off = [BASE]

        def take(shape, dt, name):
            esz = 4 if dt in (F32, I32) else 2
            free = 1
            for s in shape[1:]:
                free *= s
            b = view(off[0], shape, dt, name)
            off[0] += (free * esz + PG - 1) // PG * PG
            return b

        Wd = [take([128, 1024], BF16, f"Wd{j}") for j in range(NJ)]
        actT = [take([128, 1024], BF16, f"actT{j}") for j in range(NJ)]
        hT = take([128, 8, 1024], BF16, "hTf")
        WgS = [take([128, 8, 256], BF16, f"WgS{i}") for i in range(2)]
        WuS = [take([128, 8, 256], BF16, f"WuS{i}") for i in range(2)]
        Hbs = [take([128, 1024], BF16, f"Hbf{i}") for i in range(2)]
        tmps = [take([128, 512], F32, f"tmpf{i}") for i in range(2)]
        Gpost = take([128, 1024], F32, "Gpostf")
        hTs = take([128, 8, DS], BF16, "hTsf")
        actTs = take([128, NJ, DS], BF16, "actTsf")
        assert off[0] <= ARENA_BYTES, off[0]
        st_pre = self.statset("pre")
        st_pres = self.statset("pres")
        st_post = self.statset("post")
        which = 1 if layer == 0 else 3
        wg, wu, wd = io["w_gate"][layer], io["w_up"][layer], io["w_down"][layer]
        self.load_bcast(Gpost, io["nfq"][layer], 1024)
        X, XS = io["X"], io["XS"]

        def load_group(jg, slot):
            self.dma("pool", WgS[slot].ap, wg[:, jg * 256:(jg + 1) * 256].rearrange("(k p) n -> p k n", p=128), [], [WgS[slot]])
            self.dma("pool", WuS[slot].ap, wu[:, jg * 256:(jg + 1) * 256].rearrange("(k p) n -> p k n", p=128), [], [WuS[slot]])

        gcount = 0

        def down_tiles(tiles, lo, hi):
            for i in range(lo, hi):
                xb, n, tt = tiles[i]
                banks = [self.bank(), self.bank()]
                for h, b in enumerate(banks):
                    for j in range(NJ):
                        self.mm(PS[b][:, :], actT[j].ap[:, i * 128:(i + 1) * 128], Wd[j].ap[:, h * 512:(h + 1) * 512], j == 0, j == NJ - 1,
                                [actT[j], Wd[j]], [psk(b)])
                self.postnorm(xb, n, banks, Gpost, st_post, tmps, "dve")

        all_tiles = [[(X[8 * pas + i], 128, 8 * pas + i) for i in range(8)] for pas in range(2)]
        self.prenorm(all_tiles[0], which, hT, st_pre, Hbs, split=True)
        self.prenorm([(XS, DS, None)], which, hTs, st_pres, Hbs)
        for pas in range(2):
            tiles = all_tiles[pas]
            load_group(0, gcount % 2)
            for jg in range(NJ // 2):
                slot = gcount % 2
                gcount += 1
                if jg + 1 < NJ // 2:
                    load_group(jg + 1, gcount % 2)
                if pas == 0:
                    for j in (2 * jg, 2 * jg + 1):
                        self.dma("pool", Wd[j].ap, wd[j * 128:(j + 1) * 128, :], [], [Wd[j]])
                for jj in range(2):
                    j = 2 * jg + jj
                    for blk in range(2):
                        bg, bu = self.bank(), self.bank()
                        cs = slice(blk * 512, (blk + 1) * 512)
                        for k in range(8):
                            self.mm(PS[bg][:, :], WgS[slot].ap[:, k, jj * 128:(jj + 1) * 128], hT.ap[:, k, cs], k == 0, k == 7,
                                    [WgS[slot], hT], [psk(bg)])
                        for k in range(8):
                            self.mm(PS[bu][:, :], WuS[slot].ap[:, k, jj * 128:(jj + 1) * 128], hT.ap[:, k, cs], k == 0, k == 7,
                                    [WuS[slot], hT], [psk(bu)])
                        self.act(actT[j].ap[:, cs], PS[bg][:, :], AF.Silu, [psk(bg)], [actT[j]])
                        self.tt("dve", actT[j].ap[:, cs], actT[j].ap[:, cs], PS[bu][:, :], ALU.mult, [actT[j], psk(bu)], [actT[j]])
                    if pas == 0:
                        b = self.bank()
                        for k in range(8):
                            self.mm(PS[b][:, 0:DS], WgS[slot].ap[:, k, jj * 128:(jj + 1) * 128], hTs.ap[:, k, :], k == 0, k == 7,
                                    [WgS[slot], hTs], [psk(b)], skip=True)
                        for k in range(8):
                            self.mm(PS[b][:, DS:2 * DS], WuS[slot].ap[:, k, jj * 128:(jj + 1) * 128], hTs.ap[:, k, :], k == 0, k == 7,
                                    [WuS[slot], hTs], [psk(b)], skip=True)
                        self.act(actTs.ap[:, j, :], PS[b][:, 0:DS], AF.Silu, [psk(b)], [actTs])
                        self.tt("dve", actTs.ap[:, j, :], actTs.ap[:, j, :], PS[b][:, DS:2 * DS], ALU.mult, [actTs, psk(b)], [actTs])
            for i in range(8):
                down_tiles(tiles, i, i + 1)
                if pas == 0:
                    nx = all_tiles[1]
                    if i % 2 == 0:
                        self.prenorm_front(nx[i:i + 2], st_pre, Hbs)
                    else:
                        self.prenorm_back(nx[i - 1:i + 1], which, hT, Hbs, [(i - 1) * 128, i * 128])
            if pas == 0:
                banks = [self.bank(), self.bank()]
                for h, b in enumerate(banks):
                    for j in range(NJ):
                        self.mm(PS[b][:DS, :], actTs.ap[:, j, :], Wd[j].ap[:, h * 512:(h + 1) * 512], j == 0, j == NJ - 1,
                                [actTs, Wd[j]], [psk(b)])
                self.postnorm(XS, DS, banks, Gpost, st_post, tmps, "dve")

    def phase_C(self, BASE):
        io, c, view, PS, psk = self.io, self.consts, self.view, self.PS, self.psk
        off = [BASE]

        def take(shape, dt, name, at=None):
            esz = 4 if dt in (F32, I32) else 2
            free = 1
            for s in shape[1:]:
                free *= s
            o = off[0] if at is None else at
            b = view(o, shape, dt, name)
            if at is None:
                off[0] += (free * esz + PG - 1) // PG * PG
            return b

        KT = [take([128, S], BF16, f"KT{hp}") for hp in range(8)]
        KTall = view(BASE, [128, 8, S], BF16, "KTall")
        V16 = [take([128, 1024], BF16, f"V16_{t}") for t in range(NTT)]
        ht_off = off[0]
        hT = take([128, 8, 512], BF16, "hTc")
        Hbs = [take([128, 1024], BF16, f"Hbc{i}") for i in range(2)]
        hb0_off = off[0] - 4096
        p1 = off[0]
        w_off = BASE + 90112
        assert p1 + 12288 <= w_off
        W1 = view(w_off, [128, 8, 1024], BF16, "W1")
        W2 = view(w_off + 16384, [128, 8, 1024], BF16, "W2")
        p2 = w_off + 32768
        hT2 = view(p1, [128, 8, 512], BF16, "hTc2")
        KVf = [view(p1 + 8192, [128, 1024], F32, "KVf0"), view(p2, [128, 1024], F32, "KVf1"), view(p2 + 4096, [128, 1024], F32, "KVf2")]
        K16s = [view(p2 + 8192, [128, 1024], BF16, "K16_0")]
        c1_end = p2 + 10240
        off[0] = p1
        oT = take([128, 8, 512], BF16, "oT")
        QTs = [take([128, 512], BF16, f"QTs{i}") for i in range(2)]
        L16 = [take([128, 512], BF16, f"L16_{i}") for i in range(2)]
        assert off[0] <= w_off
        off[0] = p2
        R16 = [take([128, 512], BF16, f"R16_{i}") for i in range(2)]
        Aexp = [take([128, 512], BF16, f"Aexp{i}") for i in range(2)]
        tmp_off = off[0]
        tmps = [take([128, 512], F32, f"tmpc{i}") for i in range(2)]
        Gpost = take([128, 1024], F32, "Gpostc")
        assert off[0] <= ARENA_BYTES - 2560, off[0]
        off[0] = ARENA_BYTES - 2560
        assert off[0] >= c1_end
        QKTs = take([128, 2, 8, DS], BF16, "QKTs")
        Vs16 = take([128, 1024], BF16, "Vs16")
        assert off[0] <= ARENA_BYTES, off[0]
        Lp = [L16, [view(hb0_off, [128, 512], BF16, "L16_2"), view(hb0_off + 1024, [128, 512], BF16, "L16_3")]]
        Ap = [Aexp, [view(hb0_off + 2048, [128, 512], BF16, "Aexp2"), view(hb0_off + 3072, [128, 512], BF16, "Aexp3")]]
        E = view(tmp_off + 2048, [128, 512], F32, "E")
        Ep = [E, view(tmp_off, [128, 512], F32, "E1")]
        qt0 = ht_off
        Kc16 = [view(qt0, [128, 1024], BF16, "Kc16a"), Hbs[0]]
        Vc16 = [view(qt0 + 2048, [128, 1024], BF16, "Vc16a"), Hbs[1], view(tmp_off, [128, 1024], BF16, "Vc16c")]
        KTc = [view(qt0 + 4096, [128, 8, 128], BF16, "KTc0"), view(qt0 + 6144, [128, 8, 128], BF16, "KTc1")]
        st_pre = self.statset("pre")
        st_pres = self.statset("pres")
        st_post = self.statset("post")
        X, XS = io["X"], io["XS"]
        wq, wk, wv = io["w_qkv"][:, 0:1024], io["w_qkv"][:, 1024:2048], io["w_qkv"][:, 2048:3072]
        identb, negT, negones, tri = c["identb"], c["negT"], c["negones"], c["tri"]

        self.load_w8(W1, wk, 1024)
        self.load_w8(W2, wv, 1024)

        kvc = [0]
        kvf_rr = [0]

        def kv_tile(hTb, n, col0, kdst, vdst, kt_out, v16_out, parts=(0, 1)):
            par = kvc[0] % 2
            kvc[0] += 1
            K16 = K16s[0]
            for which_w, Wb, dst in ((0, W1, kdst), (1, W2, vdst)):
                if which_w not in parts:
                    continue
                stg = KVf[kvf_rr[0] % 3]
                kvf_rr[0] += 1
                for h in range(2):
                    b = self.bank()
                    hs = slice(h * 512, (h + 1) * 512)
                    for k in range(8):
                        self.mm(PS[b][:n, :], hTb.ap[:, k, col0:col0 + n], Wb.ap[:, k, hs], k == 0, k == 7, [hTb, Wb], [psk(b)])
                    self.cp("act", stg.ap[:n, hs], PS[b][:n, :], [psk(b)], [stg])
                    if which_w == 0:
                        self.cp("dve", K16.ap[:n, hs], PS[b][:n, :], [psk(b)], [K16])
                    else:
                        self.cp("dve", v16_out[0][:, hs], PS[b][:n, :], [psk(b)], v16_out[1])
                self.dma("sp", dst, stg.ap[:n, :], [stg], [])
            if 0 in parts:
                b = self.bank()
                pv = PS[b][:].bitcast(BF16).rearrange("p (k t) -> p k t", k=8)
                for hp in range(8):
                    self.tr(pv[:, hp, 0:n], K16.ap[:n, hp * 128:(hp + 1) * 128], identb.ap[:n, :n], [K16, identb], [psk(b)])
                self.cp("dve", kt_out[0], pv[:, :, 0:n], [psk(b)], kt_out[1])

        hTl = [hT, hT2]
        self.prenorm([(XS, DS, None)], 2, hT2, st_pres, Hbs)
        self.prenorm(self.groups[0]["tiles"], 2, hTl[0], st_pre, Hbs)
        kv_tile(hT2, DS, 0, io["k_s"][:, :], io["v_s"][:, :], (QKTs.ap[:, 1, :, :], [QKTs]), (Vs16.ap[:DS, :], [Vs16]))
        for g in range(4):
            tiles = self.groups[g]["tiles"]
            nxt = self.groups[g + 1]["tiles"] if g + 1 < 4 else None
            hTn = hTl[(g + 1) % 2]
            if g == 3:
                for parts in ((0,), (1,)):
                    for i, (xb, n, tt) in enumerate(tiles):
                        kv_tile(hTl[g % 2], 128, i * 128, io["k_p"][tt * 128:(tt + 1) * 128, :], io["v_p"][tt * 128:(tt + 1) * 128, :],
                                (KTall.ap[:, :, tt * 128:(tt + 1) * 128], KT), (V16[tt].ap[:, :], [V16[tt]]), parts=parts)
                    if parts == (0,):
                        self.load_w8(W1, wq, 1024)
                continue
            for i, (xb, n, tt) in enumerate(tiles):
                kv_tile(hTl[g % 2], 128, i * 128, io["k_p"][tt * 128:(tt + 1) * 128, :], io["v_p"][tt * 128:(tt + 1) * 128, :],
                        (KTall.ap[:, :, tt * 128:(tt + 1) * 128], KT), (V16[tt].ap[:, :], [V16[tt]]))
                if nxt is not None:
                    if i == 0:
                        self.prenorm_front(nxt[0:2], st_pre, Hbs)
                    elif i == 1:
                        self.prenorm_back(nxt[0:2], 2, hTn, Hbs, [0, 128])
                        self.prenorm_front(nxt[2:4], st_pre, Hbs)
                    elif i == 2:
                        self.prenorm_back(nxt[2:4], 2, hTn, Hbs, [256, 384])

        self.load_w8(W2, io["w_o"], 1024)
        self.load_bcast(Gpost, io["nmq"][1], 1024)
        zpool = [0, 1, 2, 3]
        for QB in range(4):
            tiles = self.groups[QB]["tiles"]
            self.bank_pool = zpool
            def qproj(hp):
                b = self.bank()
                for k in range(8):
                    self.mm(PS[b][:, :], W1.ap[:, k, hp * 128:(hp + 1) * 128], hT.ap[:, k, :], k == 0, k == 7, [W1, hT], [psk(b)])
                self.act(QTs[hp % 2].ap[:, :], PS[b][:, :], AF.Copy, [psk(b)], [QTs[hp % 2]], scale=0.125)

            if QB == 0:
                self.prenorm(tiles, 2, hT, st_pre, Hbs)
                qproj(0)
            kmax = 4 * QB + 3
            self.bank_pool = [0, 1, 2, 3, 6, 7]
            its = [(hp, kb) for hp in range(8) for kb in range(kmax, -1, -1)]
            nit = len(its)
            zb = {}

            def c0_of(kb):
                r = kb - 4 * QB
                return r * 128 if r > 0 else 0

            def stage1(ix):
                hp, kb = its[ix]
                c0 = c0_of(kb)
                zs = (self.bank(), self.bank())
                zb[ix] = zs
                for hh in range(2):
                    rows = slice(hh * 64, (hh + 1) * 64)
                    self.mm(PS[zs[hh]][:, c0:512], KT[hp].ap[rows, kb * 128:(kb + 1) * 128], QTs[hp % 2].ap[rows, c0:512], True, True,
                            [KT[hp], QTs[hp % 2]], [psk(zs[hh])], skip=True)
                if kb == kmax and hp + 1 < 8:
                    qproj(hp + 1)
                for hh in range(2):
                    self.act(Ep[hh].ap[:, c0:512], PS[zs[hh]][:, c0:512], AF.Exp, [psk(zs[hh])], [Ep[hh]])
                for hh in range(2):
                    z = zs[hh]
                    L = Lp[ix % 2][hh]
                    Eb = Ep[hh]
                    self.act(L.ap[:, c0:512], Eb.ap[:, c0:512], AF.Ln, [Eb], [L], bias=1.0)
                    if kb >= 4 * QB:
                        self.tt("dve", L.ap[:, c0:c0 + 128], L.ap[:, c0:c0 + 128], tri.ap[:, :], ALU.mult, [L, tri], [L])

            def stage2(ix):
                hp, kb = its[ix]
                c0 = c0_of(kb)
                first = kb == kmax
                if first:
                    for hh in range(2):
                        self.memset("pool", R16[hh].ap[:, :], 0.0, [R16[hh]])
                for hh in range(2):
                    z = zb[ix][hh]
                    L = Lp[ix % 2][hh]
                    self.mm(PS[z][:, c0:512], negT.ap[:, :], L.ap[:, c0:512], False, first, [negT, L], [psk(z)], skip=True)
                    if not first:
                        self.mm(PS[z][:, c0:512], negones.ap[:, :], R16[hh].ap[:, c0:512], False, True, [negones, R16[hh]], [psk(z)], skip=True)
                for hh in range(2):
                    z = zb[ix][hh]
                    L = Lp[ix % 2][hh]
                    A = Ap[ix % 2][hh]
                    if kb > 0:
                        self.tt("dve", R16[hh].ap[:, c0:512], R16[hh].ap[:, c0:512], L.ap[:, c0:512], ALU.add, [R16[hh], L], [R16[hh]])
                    self.act(A.ap[:, c0:512], PS[z][:, c0:512], AF.Exp, [psk(z)], [A])
                    if kb >= 4 * QB:
                        self.tt("dve", A.ap[:, c0:c0 + 128], A.ap[:, c0:c0 + 128], tri.ap[:, :], ALU.mult, [A, tri], [A])

            def stage3(ix):
                hp, kb = its[ix]
                c0 = c0_of(kb)
                ob = 4 + (hp % 2)
                for hh in range(2):
                    A = Ap[ix % 2][hh]
                    rows = slice(hh * 64, (hh + 1) * 64)
                    cols = slice(hp * 128 + hh * 64, hp * 128 + (hh + 1) * 64)
                    self.mm(PS[ob][rows, c0:512], V16[kb].ap[:, cols], A.ap[:, c0:512], kb == kmax, kb == 0,
                            [V16[kb], A], [psk(ob)], skip=True)
                if kb == 0:
                    self.cp("dve", oT.ap[:, hp, :], PS[ob][:, :], [psk(ob)], [oT])

            for s_ in range(nit + 2):
                if s_ < nit:
                    stage1(s_)
                if 0 <= s_ - 1 < nit:
                    stage2(s_ - 1)
                if 0 <= s_ - 2 < nit:
                    stage3(s_ - 2)
            self.bank_pool = zpool
            nxt_t = self.groups[QB + 1]["tiles"] if QB + 1 < 4 else None
            if nxt_t is not None:
                self.prenorm_front(nxt_t[0:2], st_pre, Hbs)
            self.bank_pool = list(range(8))
            self.bank_rr = 0
            tmps4 = tmps + [view(p2, [128, 512], F32, "R16asF32"), view(p2 + 2048, [128, 512], F32, "AexpAsF32")]
            for p0 in range(0, 4, 2):
                items = []
                for i in range(p0, p0 + 2):
                    xb, n, tt = tiles[i]
                    banks = [self.bank(), self.bank()]
                    for h, b in enumerate(banks):
                        for hp in range(8):
                            self.mm(PS[b][:, :], oT.ap[:, hp, i * 128:(i + 1) * 128], W2.ap[:, hp, h * 512:(h + 1) * 512], hp == 0, hp == 7,
                                    [oT, W2], [psk(b)])
                    items.append((xb, n, banks))
                if nxt_t is not None:
                    if p0 == 0:
                        self.prenorm_back(nxt_t[0:2], 2, hT, Hbs, [0, 128])
                        self.prenorm_front(nxt_t[2:4], st_pre, Hbs)
                    else:
                        self.prenorm_back(nxt_t[2:4], 2, hT, Hbs, [256, 384])
                self.postnorm_multi(items, Gpost, st_post, tmps4, "dve")
            if nxt_t is not None:
                qproj(0)
            self.bank_pool = zpool

        self.bank_pool = zpool
        self.prenorm([(XS, DS, None)], 2, hT, st_pres, [Hbs[0]])
        b = self.bank()
        pvq = PS[b][:, 0:128].rearrange("p (h q) -> p h q", h=8)
        for hp in range(8):
            for k in range(8):
                self.mm(pvq[:, hp, :], W1.ap[:, k, hp * 128:(hp + 1) * 128], hT.ap[:, k, 0:DS], k == 0, k == 7, [W1, hT], [psk(b)], skip=True)
        self.act(QKTs.ap[:, 0, :, :], pvq, AF.Copy, [psk(b)], [QKTs], scale=0.125)
        QZb = R16[1].sub(R16[1].ap[:, 0:256].rearrange("p (a b q) -> p a b q", a=8, b=2))
        self.memset("pool", R16[1].ap[:, 0:256], 0.0, [R16[1]])
        for hh in range(2):
            rows = slice(hh * 64, (hh + 1) * 64)
            self.cp("pool", QZb.ap[rows, :, hh, :], QKTs.ap[rows, 0, :, :], [QKTs, R16[1]], [R16[1]])
        ob = 4
        self.memset("dve", PS[ob][:, 0:256], 0.0, [psk(ob)])
        self.memset("pool", R16[0].ap[:, 0:256], 0.0, [R16[0]])
        blocks = [None] + list(range(15, -1, -1))
        nblk = len(blocks)
        zb = {}
        Ls, As_ = L16, Aexp
        HW = 16 * DS

        def s_kv(ix):
            kb = blocks[ix]
            if kb is None:
                return DS, QKTs.ap[:, 1, :, :], QKTs, Vs16
            return 128, KTc[ix % 2].ap, KTc[ix % 2], Vc16[ix % 3]

        def s_stage0(ix):
            kb = blocks[ix]
            if kb is None:
                return
            kc, vc, ktc = Kc16[ix % 2], Vc16[ix % 3], KTc[ix % 2]
            self.dma("pool", kc.ap[:, :], io["ck"][kb * 128:(kb + 1) * 128, :], [], [kc])
            self.dma("pool", vc.ap[:, :], io["cv"][kb * 128:(kb + 1) * 128, :], [], [vc])
            b_ = self.bank()
            pv = PS[b_][:].bitcast(BF16).rearrange("p (k t) -> p k t", k=8)
            for hp in range(8):
                self.tr(pv[:, hp, :], kc.ap[:, hp * 128:(hp + 1) * 128], identb.ap[:, :], [kc, identb], [psk(b_)])
            self.cp("dve", ktc.ap[:, :, :], pv[:, :, :], [psk(b_)], [ktc])

        def s_stage1(ix):
            nk, ktap, ktbuf, vbuf = s_kv(ix)
            z = self.bank()
            zb[ix] = z
            self.memset("dve", PS[z][:, 0:HW], 0.0, [psk(z)])
            for h in range(16):
                self.mm(PS[z][:nk, h * DS:(h + 1) * DS], ktap[:, h // 2, 0:nk], QZb.ap[:, h // 2, h % 2, :], False, False,
                        [ktbuf, R16[1]], [psk(z)], skip=True)
            self.act(E.ap[:nk, 0:HW], PS[z][:nk, 0:HW], AF.Exp, [psk(z)], [E])
            L = Ls[ix % 2]
            self.act(L.ap[:nk, 0:HW], E.ap[:nk, 0:HW], AF.Ln, [E], [L], bias=1.0)
            if blocks[ix] is None:
                lv = L.ap[:nk, 0:HW].rearrange("p (h q) -> p h q", h=16)
                self.tt("dve", lv, lv, tri.ap[:DS, :DS].unsqueeze(1).to_broadcast([DS, 16, DS]), ALU.mult, [L, tri], [L])

        def s_stage2(ix):
            nk, ktap, ktbuf, vbuf = s_kv(ix)
            z = zb[ix]
            L = Ls[ix % 2]
            A = As_[ix % 2]
            self.mm(PS[z][:nk, 0:HW], negT.ap[:nk, :nk], L.ap[:nk, 0:HW], False, ix == 0, [negT, L], [psk(z)], skip=True)
            if ix > 0:
                self.mm(PS[z][:nk, 0:HW], negones.ap[:, :nk], R16[0].ap[:, 0:HW], False, True, [negones, R16[0]], [psk(z)], skip=True)
            if ix + 1 < nblk:
                self.tt("dve", R16[0].ap[:nk, 0:HW], R16[0].ap[:nk, 0:HW], L.ap[:nk, 0:HW], ALU.add, [R16[0], L], [R16[0]])
            self.act(A.ap[:nk, 0:HW], PS[z][:nk, 0:HW], AF.Exp, [psk(z)], [A])
            if blocks[ix] is None:
                av = A.ap[:nk, 0:HW].rearrange("p (h q) -> p h q", h=16)
                self.tt("dve", av, av, tri.ap[:DS, :DS].unsqueeze(1).to_broadcast([DS, 16, DS]), ALU.mult, [A, tri], [A])

        def s_stage3(ix):
            nk, ktap, ktbuf, vbuf = s_kv(ix)
            A = As_[ix % 2]
            for h in range(16):
                hp = h // 2
                self.mm(PS[ob][:, h * DS:(h + 1) * DS], vbuf.ap[:nk, hp * 128:(hp + 1) * 128], A.ap[:nk, h * DS:(h + 1) * DS], False, False,
                        [vbuf, A], [psk(ob)], skip=True)

        for s_ in range(nblk + 3):
            if 0 <= s_ - 3 < nblk:
                s_stage3(s_ - 3)
            if 0 <= s_ - 2 < nblk:
                s_stage2(s_ - 2)
            if 0 <= s_ - 1 < nblk:
                s_stage1(s_ - 1)
            if s_ < nblk:
                s_stage0(s_)
        ov = PS[ob][:, 0:HW].rearrange("p (a b q) -> p a b q", a=8, b=2)
        for hh in range(2):
            rows = slice(hh * 64, (hh + 1) * 64)
            self.cp("dve", oT.ap[rows, :, 0:DS], ov[rows, :, hh, :], [psk(ob)], [oT])
        banks = [self.bank(), self.bank()]
        for h, b in enumerate(banks):
            for hp in range(8):
                self.mm(PS[b][:DS, :], oT.ap[:, hp, 0:DS], W2.ap[:, hp, h * 512:(h + 1) * 512], hp == 0, hp == 7, [oT, W2], [psk(b)])
        self.postnorm(XS, DS, banks, Gpost, st_post, tmps, "pool")
        self.bank_pool = list(range(8))


def _consts():
    i = np.arange(128)
    c = {}
    c["c_ident"] = np.eye(128, dtype=np.float32)
    c["c_negT"] = -(i[:, None] >= i[None, :]).astype(np.float32)
    c["c_negones"] = -np.ones((128, 128), np.float32)
    c["c_tri"] = (i[:, None] < i[None, :]).astype(np.float32)
    c["c_cmask"] = ((i[:, None] // 64) <= (i[None, :] // 64)).astype(np.float32)
    inv = np.zeros((4, 16), np.float32)
    for g, w in enumerate(POOL_W):
        inv[g] = 1.0 / np.minimum(w, np.arange(16) + 1)
    c["c_invc"] = inv.reshape(64)
    c["c_onesrow"] = np.ones((1, 128), np.float32)
    return c


def make_in_maps(inp):
    f = lambda a: np.ascontiguousarray(np.asarray(a, dtype=np.float32))
    shared = {
        "nmp": f(inp["norm_mix_pre"]), "nmq": f(inp["norm_mix_post"]), "nfp": f(inp["norm_ffn_pre"]), "nfq": f(inp["norm_ffn_post"]),
        "w_in": f(inp["w_in_ab"])[0], "ln_g": f(inp["ln_v_g"])[0], "ln_b": f(inp["ln_v_b"])[0],
        "w_sp": f(inp["w_spatial"])[0], "b_sp": f(inp["b_spatial"])[0], "w_map": f(inp["w_pool_map"])[0],
        "p_scale": f(inp["pool_scale"])[0], "w_out": f(inp["w_out_ab"])[0], "w_qkv": f(inp["w_qkv"])[0], "w_o": f(inp["w_o_sb"])[0],
        "w_gate": f(inp["w_gate"]), "w_up": f(inp["w_up"]), "w_down": f(inp["w_down"]),
    }
    shared.update(_consts())
    xp = f(inp["x_prompt"]); xs = f(inp["x_sample"]); sp = f(inp["state_pool"]); ck = f(inp["cache_k"]); cv = f(inp["cache_v"])
    maps = []
    for b in range(8):
        m = dict(shared)
        m["xp"] = xp[b]
        m["xs"] = xs[b]
        m["spool"] = sp[0, b]
        m["ck"] = ck[0, b].reshape(S, D)
        m["cv"] = cv[0, b].reshape(S, D)
        maps.append(m)
    return maps


def assemble(res):
    g = lambda k: np.stack([np.asarray(r[k], dtype=np.float32) for r in res])
    y_p = g("y_p")
    y_s = g("y_s")
    pool_p = g("pool_p")[None]
    k_p = g("k_p").reshape(1, 8, S, 16, 64)
    v_p = g("v_p").reshape(1, 8, S, 16, 64)
    pool_s = g("pool_s")[None]
    k_s = g("k_s").reshape(1, 8, DS, 16, 64)
    v_s = g("v_s").reshape(1, 8, DS, 16, 64)
    vn_s = g("vn_s")[None]
    return (y_p, y_s, pool_p, k_p, v_p, pool_s, k_s, v_s, vn_s)


def kernel(**inputs):
    nc = Builder().build()
    maps = make_in_maps(inputs)
    res = run_bass_kernel_spmd(nc, maps, core_ids=list(range(8)))
    return assemble(res.results)
```

```python
import numpy as np
from contextlib import ExitStack
import concourse.bass as bass
import concourse.mybir as mybir
from concourse.bass_utils import run_bass_kernel_spmd

F32 = mybir.dt.float32
BF16 = mybir.dt.bfloat16
I32 = mybir.dt.int32
AF = mybir.ActivationFunctionType
ALU = mybir.AluOpType

D = 1024
S = 2048
NTT = 16
DS = 16
DFF = 2816
NJ = 22
EPS = 1e-6
POOL_W = (2, 4, 8, 16)
NR_STEPS = 2

ENGS = ("pe", "act", "dve", "pool", "sp")
NDSEM = 12
PG = 512


class Buf:
    def __init__(self, ap, keys, name=""):
        self.ap = ap
        self.keys = tuple(keys)
        self.name = name

    def sub(self, ap, keys=None):
        return Buf(ap, self.keys if keys is None else keys, self.name)


def _keys(items):
    out = []
    for it in items:
        if it is None:
            continue
        if isinstance(it, Buf):
            out.extend(it.keys)
        elif isinstance(it, (list, tuple)) and it and isinstance(it[0], (Buf, list)):
            out.extend(_keys(it))
        else:
            out.append(it)
    return out


class Prog:
    def __init__(self):
        self.ops = []

    def add(self, eng, fn, reads=(), writes=(), dma=False):
        self.ops.append(dict(eng=eng, fn=fn, reads=tuple(dict.fromkeys(_keys(reads))),
                             writes=tuple(dict.fromkeys(_keys(writes))), dma=dma))

    def resolve(self):
        last_w = {}
        readers = {}
        ps_last = {}
        ops = self.ops
        for i, op in enumerate(ops):
            deps = set()
            e = op["eng"]
            for r in op["reads"]:
                if r in last_w:
                    deps.add(last_w[r])
            for w in op["writes"]:
                if w in last_w:
                    deps.add(last_w[w])
                for rd in readers.get(w, ()):
                    deps.add(rd)
            for k in op["reads"] + op["writes"]:
                if isinstance(k, tuple) and k and k[0] == "ps":
                    la = ps_last.setdefault(k, {})
                    for e2, j in la.items():
                        if e2 != e:
                            deps.add(j)
                    la[e] = i
            deps.discard(i)
            keep = set()
            for d in deps:
                dop = ops[d]
                if dop["eng"] == e and e == "pe" and not dop["dma"] and not op["dma"]:
                    continue
                keep.add(d)
            op["deps"] = keep
            for w in op["writes"]:
                last_w[w] = i
                readers[w] = []
            for r in op["reads"]:
                if r not in op["writes"]:
                    readers.setdefault(r, []).append(i)
        needed = set()
        for op in ops:
            needed |= op["deps"]
        self.needed = needed

    def emit(self, block, sems, dsems):
        self.resolve()
        ops = self.ops
        cnt = {e: 0 for e in ENGS}
        dcnt = {e: 0 for e in ENGS}
        dval = {e: [0] * NDSEM for e in ENGS}
        for i, op in enumerate(ops):
            e = op["eng"]
            if op["dma"]:
                k = dcnt[e] % NDSEM
                dcnt[e] += 1
                op["prev_tok"] = (("d", e, k), dval[e][k])
                dval[e][k] += 16
                op["tok"] = (("d", e, k), dval[e][k])
            elif i in self.needed:
                cnt[e] += 1
                op["tok"] = (("c", e), cnt[e])
            else:
                op["tok"] = None
        per_eng = {e: [] for e in ENGS}
        for i, op in enumerate(ops):
            per_eng[op["eng"]].append(i)

        def semof(key):
            return sems[key[1]] if key[0] == "c" else dsems[key[1]][key[2]]

        final_tokens = {}
        for op in ops:
            if op["dma"]:
                final_tokens[op["tok"][0]] = op["tok"][1]

        def make_stream(e):
            def stream(eng):
                waited = {}
                for i in per_eng[e]:
                    op = ops[i]
                    want = {}
                    for d in op["deps"]:
                        key, val = ops[d]["tok"]
                        if want.get(key, 0) < val:
                            want[key] = val
                    if op["dma"]:
                        key, val = op["prev_tok"]
                        if val > 0 and want.get(key, 0) < val:
                            want[key] = val
                    for key, val in want.items():
                        if waited.get(key, 0) >= val:
                            continue
                        eng.wait_ge(semof(key), val)
                        waited[key] = val
                    ins = op["fn"](eng)
                    if op["tok"] is not None:
                        ins.then_inc(semof(op["tok"][0]), 16 if op["dma"] else 1)
                if e == "sp":
                    for key, val in final_tokens.items():
                        if waited.get(key, 0) < val:
                            eng.wait_ge(semof(key), val)
            return stream

        block.tensor(make_stream("pe"))
        block.scalar(make_stream("act"))
        block.vector(make_stream("dve"))
        block.gpsimd(make_stream("pool"))
        block.sync(make_stream("sp"))


ARENA_BYTES = 210944
PERSIST_BYTES = 73216


class Builder:
    def __init__(self, stop_after=None):
        self.stop_after = stop_after
        self.nc = bass.Bass("TRN2", target_bir_lowering=False)
        self.P = Prog()
        self.es = ExitStack()
        self.dram = {}
        self.bank_rr = 0
        self.bank_pool = list(range(8))
        self.uid = 0

    def din(self, name, shape, dt=F32):
        t = self.nc.dram_tensor(name, list(shape), dt, kind="ExternalInput").ap()
        self.dram[name] = t
        return t

    def dout(self, name, shape, dt=F32):
        t = self.nc.dram_tensor(name, list(shape), dt, kind="ExternalOutput").ap()
        self.dram[name] = t
        return t

    def view(self, off, shape, dt, name=""):
        esz = 4 if dt in (F32, I32) else 2
        free = 1
        for s in shape[1:]:
            free *= s
        nbytes = free * esz
        assert off % 4 == 0 and off + nbytes <= ARENA_BYTES, (name, off, nbytes)
        ap = self.arena[0:shape[0], off // 2:(off + nbytes) // 2]
        if esz == 4:
            ap = ap.bitcast(dt)
        if len(shape) == 3:
            ap = ap.rearrange("p (a b) -> p a b", a=shape[1])
        elif len(shape) == 4:
            ap = ap.rearrange("p (a b c) -> p a b c", a=shape[1], b=shape[2])
        keys = [("pg", i) for i in range(off // PG, (off + nbytes + PG - 1) // PG)]
        return Buf(ap, keys, name)

    def small(self, name, shape, dt=F32):
        t = self.es.enter_context(self.nc.sbuf_tensor(name, list(shape), dt))
        return Buf(t[:], [("sm", name)], name)

    def bank(self):
        b = self.bank_pool[self.bank_rr % len(self.bank_pool)]
        self.bank_rr += 1
        return b

    def psk(self, b):
        return ("ps", b)

    def mm(self, out, lhsT, rhs, start, stop, reads, writes, skip=False):
        if skip:
            fn = lambda t: t.matmul(out, lhsT=lhsT, rhs=rhs, start=start, stop=stop, skip_group_check=True)
        else:
            fn = lambda t: t.matmul(out, lhsT=lhsT, rhs=rhs, start=start, stop=stop)
        self.P.add("pe", fn, reads, writes)

    def tr(self, out, in_, ident, reads, writes):
        self.P.add("pe", lambda t: t.transpose(out=out, in_=in_, identity=ident), reads, writes)

    def act(self, out, in_, func, reads, writes, bias=None, scale=None, accum_out=None):
        kw = {}
        if bias is not None:
            kw["bias"] = bias
        if scale is not None:
            kw["scale"] = scale
        if accum_out is not None:
            kw["accum_out"] = accum_out
        self.P.add("act", lambda a: a.activation(out=out, in_=in_, func=func, **kw), reads, writes)

    def tt(self, eng, out, in0, in1, op, reads, writes):
        self.P.add(eng, lambda v: v.tensor_tensor(out=out, in0=in0, in1=in1, op=op), reads, writes)

    def ts(self, out, in0, s1, s2, op0, op1, reads, writes):
        if op1 is None:
            self.P.add("dve", lambda v: v.tensor_scalar(out=out, in0=in0, scalar1=s1, scalar2=None, op0=op0), reads, writes)
        else:
            self.P.add("dve", lambda v: v.tensor_scalar(out=out, in0=in0, scalar1=s1, scalar2=s2, op0=op0, op1=op1), reads, writes)

    def stt(self, out, in0, scalar, in1, op0, op1, reads, writes):
        self.P.add("dve", lambda v: v.scalar_tensor_tensor(out=out, in0=in0, scalar=scalar, in1=in1, op0=op0, op1=op1), reads, writes)

    def cp(self, eng, out, in_, reads, writes):
        if eng == "act":
            self.P.add("act", lambda a: a.copy(out=out, in_=in_), reads, writes)
        else:
            self.P.add(eng, lambda v: v.tensor_copy(out=out, in_=in_), reads, writes)

    def memset(self, eng, ap, val, writes):
        self.P.add(eng, lambda v: v.memset(ap, val), (), writes)

    def dma(self, q, out, in_, reads, writes, slow=False):
        if slow:
            self.P.add(q, lambda e: e.dma_start(out=out, in_=in_, allow_slow_non_contiguous=True), reads, writes, dma=True)
        else:
            self.P.add(q, lambda e: e.dma_start(out=out, in_=in_), reads, writes, dma=True)

    def rsqrt(self, st, n, m, src, add_eps=True):
        a = st["a"].ap[:n, 0:m]
        y = st["y"].ap[:n, 0:m]
        t1 = st["t1"].ap[:n, 0:m]
        t2 = st["t2"].ap[:n, 0:m]
        A, Y, T1, T2 = st["a"], st["y"], st["t1"], st["t2"]
        self.ts(a, src[0], EPS, None, ALU.add, None, [src[1]], [A])
        self.ts(y.bitcast(I32), a.bitcast(I32), 1, None, ALU.arith_shift_right, None, [A], [Y])
        self.ts(y.bitcast(I32), y.bitcast(I32), -1, 0x5F3759DF, ALU.mult, ALU.add, [Y], [Y])
        for _ in range(NR_STEPS):
            self.tt("dve", t1, a, y, ALU.mult, [A, Y], [T1])
            self.stt(t2, t1, -0.5, y, ALU.mult, ALU.mult, [T1, Y], [T2])
            self.stt(y, t2, 1.5, y, ALU.add, ALU.mult, [T2, Y], [Y])

    def statset(self, name, m=8):
        if not hasattr(self, "_stat_t"):
            self._stat_t = self.es.enter_context(self.nc.sbuf_tensor("stats", [128, 448], F32))
            self._stat_off = 0
            self._stat_sets = {}
        if name in self._stat_sets:
            return self._stat_sets[name]
        out = {}
        for k in ("ss", "s2", "a", "y", "t1", "t2", "mu"):
            o = self._stat_off
            self._stat_off += m
            assert self._stat_off <= 448
            out[k] = Buf(self._stat_t[:, o:o + m], [("sm", f"{name}_{k}")], f"{name}_{k}")
        self._stat_sets[name] = out
        return out

    def build(self):
        nc, es = self.nc, self.es
        din, dout = self.din, self.dout
        xp = din("xp", [S, D]); xs = din("xs", [DS, D]); spool = din("spool", [15, 512])
        ck = din("ck", [S, D]); cv = din("cv", [S, D])
        nmp = din("nmp", [2, D]); nmq = din("nmq", [2, D]); nfp = din("nfp", [2, D]); nfq = din("nfq", [2, D])
        w_in = din("w_in", [D, 1536]); ln_g = din("ln_g", [512]); ln_b = din("ln_b", [512])
        w_sp = din("w_sp", [4, 128, 128]); b_sp = din("b_sp", [4, 128]); w_map = din("w_map", [4, 128, 128])
        p_scale = din("p_scale", [512]); w_out = din("w_out", [D, D])
        w_qkv = din("w_qkv", [D, 3 * D]); w_o = din("w_o", [D, D])
        w_gate = din("w_gate", [2, D, DFF]); w_up = din("w_up", [2, D, DFF]); w_down = din("w_down", [2, DFF, D])
        c_ident = din("c_ident", [128, 128]); c_negT = din("c_negT", [128, 128]); c_negones = din("c_negones", [128, 128])
        c_tri = din("c_tri", [128, 128]); c_cmask = din("c_cmask", [128, 128]); c_invc = din("c_invc", [64])
        c_onesrow = din("c_onesrow", [1, 128])
        y_p = dout("y_p", [S, D]); y_s = dout("y_s", [DS, D]); pool_p = dout("pool_p", [15, 512])
        k_p = dout("k_p", [S, D]); v_p = dout("v_p", [S, D]); pool_s = dout("pool_s", [15, 512])
        k_s = dout("k_s", [DS, D]); v_s = dout("v_s", [DS, D]); vn_s = dout("vn_s", [DS, 512])

        arena_t = es.enter_context(nc.sbuf_tensor("arena", [128, ARENA_BYTES // 2], BF16))
        self.arena = arena_t
        PS = [es.enter_context(nc.psum_tensor(f"psb{i}", [128, 512], F32)) for i in range(8)]
        self.PS = PS
        sems = {e: es.enter_context(nc.semaphore("s_" + e)) for e in ENGS}
        dsems = {e: [es.enter_context(nc.semaphore(f"d_{e}{k}")) for k in range(NDSEM)] for e in ("sp", "pool")}
        view = self.view
        P = self.P
        psk = self.psk

        off = 0
        Xall = view(off, [128, NTT, D], F32, "X"); off += 65536
        X = [Xall.sub(Xall.ap[:, t, :], Xall.keys[t * 8:(t + 1) * 8]) for t in range(NTT)]
        XS = view(off, [128, D], F32, "XS"); off += 4096
        identb = view(off, [128, 128], BF16, "identb"); off += 512
        identf = view(off, [128, 128], F32, "identf"); off += 512
        negT = view(off, [128, 128], BF16, "negT"); off += 512
        negones = view(off, [128, 128], BF16, "negones"); off += 512
        tri = view(off, [128, 128], BF16, "tri"); off += 512
        onesrow = view(off, [128, 128], BF16, "onesrow"); off += 512
        gcol = view(off, [128, 4, 8], F32, "gcol"); off += 512
        assert off <= PERSIST_BYTES
        BASE = PERSIST_BYTES

        self.dma("pool", identb.ap, c_ident[:, :], [], [identb])
        self.dma("sp", identf.ap, c_ident[:, :], [], [identf])
        self.dma("pool", negT.ap, c_negT[:, :], [], [negT])
        self.dma("pool", negones.ap, c_negones[:, :], [], [negones])
        self.dma("pool", tri.ap, c_tri[:, :], [], [tri])
        self.dma("pool", onesrow.ap[0:1, :], c_onesrow[:, :], [], [onesrow])
        for wi, src in enumerate((nmp[0], nfp[0], nmp[1], nfp[1])):
            self.dma("sp", gcol.ap[:, wi, :], src.rearrange("(k p) -> p k", p=128), [], [gcol], slow=True)
        for t in range(4):
            self.dma("sp", X[t].ap, xp[t * 128:(t + 1) * 128, :], [], [X[t]])
        self.dma("sp", XS.ap[0:DS, :], xs[:, :], [], [XS])

        self.groups = [dict(name=f"tg{g}", tiles=[(X[4 * g + i], 128, 4 * g + i) for i in range(4)], NT=512, sample=False)
                       for g in range(4)]
        self.sgroup = dict(name="smp", tiles=[(XS, DS, None)], NT=DS, sample=True)
        self.consts = dict(identb=identb, identf=identf, negT=negT, negones=negones, tri=tri, onesrow=onesrow, gcol=gcol)
        self.io = locals()

        self.phase_A(BASE)
        if self.stop_after == "A":
            return self.finish(sems, dsems)
        self.phase_ffn(BASE, 0)
        if self.stop_after == "B":
            return self.finish(sems, dsems)
        self.phase_C(BASE)
        if self.stop_after == "C":
            return self.finish(sems, dsems)
        self.phase_ffn(BASE, 1)
        return self.finish(sems, dsems)

    def finish(self, sems, dsems):
        io = self.io
        X, XS = io["X"], io["XS"]
        for t in range(NTT):
            self.dma("sp", io["y_p"][t * 128:(t + 1) * 128, :], X[t].ap, [X[t]], [])
        self.dma("sp", io["y_s"][:, :], XS.ap[0:DS, :], [XS], [])
        block = self.es.enter_context(self.nc.Block())
        self.P.emit(block, sems, dsems)
        self.es.close()
        return self.nc

    def prenorm(self, tiles, which, hT, st, Hbs, col0=0):
        c = self.consts
        m = len(tiles)
        n0 = tiles[0][1]
        for i, (xb, n, _) in enumerate(tiles):
            hb = Hbs[i % len(Hbs)]
            self.act(hb.ap[:n, :], xb.ap[:n, :], AF.Square, [xb], [hb, st["ss"]], scale=1.0 / 32.0,
                     accum_out=st["ss"].ap[:n, i:i + 1])
        self.rsqrt(st, n0, m, (st["ss"].ap[:n0, 0:m], st["ss"]))
        for i, (xb, n, _) in enumerate(tiles):
            hb = Hbs[i % len(Hbs)]
            self.act(hb.ap[:n, :], xb.ap[:n, :], AF.Copy, [xb, st["y"]], [hb], scale=st["y"].ap[:n, i:i + 1])
            b = self.bank()
            pv = self.PS[b][:].bitcast(BF16).rearrange("p (k t) -> p k t", k=8)
            for k in range(8):
                self.tr(pv[:, k, 0:n], hb.ap[:n, k * 128:(k + 1) * 128], c["identb"].ap[:n, :n], [hb, c["identb"]], [self.psk(b)])
            c0 = col0 + i * 128
            g = c["gcol"].ap[:, which, :]
            self.tt("dve", hT.ap[:, :, c0:c0 + n], pv[:, :, 0:n], g.unsqueeze(2).to_broadcast([128, 8, n]), ALU.mult,
                    [self.psk(b), c["gcol"]], [hT])

    def prenorm_front(self, tiles, st, Hbs):
        assert len(tiles) <= len(Hbs)
        m = len(tiles)
        n0 = tiles[0][1]
        for i, (xb, n, _) in enumerate(tiles):
            self.act(Hbs[i].ap[:n, :], xb.ap[:n, :], AF.Square, [xb], [Hbs[i], st["ss"]], scale=1.0 / 32.0,
                     accum_out=st["ss"].ap[:n, i:i + 1])
        self.rsqrt(st, n0, m, (st["ss"].ap[:n0, 0:m], st["ss"]))
        for i, (xb, n, _) in enumerate(tiles):
            self.act(Hbs[i].ap[:n, :], xb.ap[:n, :], AF.Copy, [xb, st["y"]], [Hbs[i]], scale=st["y"].ap[:n, i:i + 1])

    def prenorm_back(self, tiles, which, hT, Hbs, cols):
        c = self.consts
        for i, (xb, n, _) in enumerate(tiles):
            hb = Hbs[i]
            b = self.bank()
            pv = self.PS[b][:].bitcast(BF16).rearrange("p (k t) -> p k t", k=8)
            for k in range(8):
                self.tr(pv[:, k, 0:n], hb.ap[:n, k * 128:(k + 1) * 128], c["identb"].ap[:n, :n], [hb, c["identb"]], [self.psk(b)])
            g = c["gcol"].ap[:, which, :]
            self.tt("dve", hT.ap[:, :, cols[i]:cols[i] + n], pv[:, :, 0:n], g.unsqueeze(2).to_broadcast([128, 8, n]), ALU.mult,
                    [self.psk(b), c["gcol"]], [hT])

    def postnorm(self, xb, n, banks, Gpost, st, tmps, add_eng):
        PS = self.PS
        for h, b in enumerate(banks):
            tmp = tmps[h]
            self.act(tmp.ap[:n, :], PS[b][:n, :], AF.Square, [self.psk(b)], [tmp, st["ss"]], scale=1.0 / 32.0,
                     accum_out=st["ss"].ap[:n, h:h + 1])
        self.tt("dve", st["s2"].ap[:n, 0:1], st["ss"].ap[:n, 0:1], st["ss"].ap[:n, 1:2], ALU.add, [st["ss"]], [st["s2"]])
        self.rsqrt(st, n, 1, (st["s2"].ap[:n, 0:1], st["s2"]))
        for h, b in enumerate(banks):
            tmp = tmps[h]
            self.stt(tmp.ap[:n, :], PS[b][:n, :], st["y"].ap[:n, 0:1], Gpost.ap[:n, h * 512:(h + 1) * 512], ALU.mult, ALU.mult,
                     [self.psk(b), st["y"], Gpost], [tmp])
            self.tt(add_eng, xb.ap[:n, h * 512:(h + 1) * 512], xb.ap[:n, h * 512:(h + 1) * 512], tmp.ap[:n, :], ALU.add,
                    [xb, tmp], [xb])

    def postnorm_multi(self, items, Gpost, st, tmps, add_eng):
        PS = self.PS
        m = len(items)
        n0 = items[0][1]
        for t, (xb, n, banks) in enumerate(items):
            for h, b in enumerate(banks):
                tmp = tmps[2 * t + h]
                self.act(tmp.ap[:n, :], PS[b][:n, :], AF.Square, [self.psk(b)], [tmp, st["ss"]], scale=1.0 / 32.0,
                         accum_out=st["ss"].ap[:n, 2 * t + h:2 * t + h + 1])
        self.tt("dve", st["s2"].ap[:n0, 0:m], st["ss"].ap[:n0, 0:2 * m:2], st["ss"].ap[:n0, 1:2 * m:2], ALU.add, [st["ss"]], [st["s2"]])
        self.rsqrt(st, n0, m, (st["s2"].ap[:n0, 0:m], st["s2"]))
        for t, (xb, n, banks) in enumerate(items):
            for h, b in enumerate(banks):
                tmp = tmps[2 * t + h]
                self.stt(tmp.ap[:n, :], PS[b][:n, :], st["y"].ap[:n, t:t + 1], Gpost.ap[:n, h * 512:(h + 1) * 512], ALU.mult, ALU.mult,
                         [self.psk(b), st["y"], Gpost], [tmp])
                self.tt(add_eng, xb.ap[:n, h * 512:(h + 1) * 512], xb.ap[:n, h * 512:(h + 1) * 512], tmp.ap[:n, :], ALU.add,
                        [xb, tmp], [xb])

    def load_w8(self, dst, src2d, ncols):
        for k in range(8):
            self.dma("pool", dst.ap[:, k, :], src2d[k * 128:(k + 1) * 128, :], [], [dst])

    def load_bcast(self, dst, vec, n):
        self.dma("sp", dst.ap[:, 0:n], vec.partition_broadcast(128), [], [dst])

    def phase_A(self, BASE):
        io, c, view, PS, psk = self.io, self.consts, self.view, self.PS, self.psk
        nc = self.nc
        off = [BASE]

        def take(shape, dt, name):
            esz = 4 if dt in (F32, I32) else 2
            free = 1
            for s in shape[1:]:
                free *= s
            b = view(off[0], shape, dt, name)
            off[0] += (free * esz + PG - 1) // PG * PG
            return b

        Win = take([128, 8, 1536], BF16, "Win")
        Wout = take([128, 8, 1024], BF16, "Wout")
        hTs = [take([128, 8, 512], BF16, f"hT{i}") for i in range(2)]
        uTs = [take([128, 4, 512], BF16, f"uT{i}") for i in range(2)]
        pTs = [take([128, 4, 528], F32, f"pT{i}") for i in range(2)]
        vfs = [take([128, 512], F32, f"vf{i}") for i in range(4)]
        vn16s = [take([128, 512], BF16, f"vn16{i}") for i in range(4)]
        vnf = vfs[1]
        abT = take([128, 8, 512], BF16, "abT")
        dT = take([128, 4, 512], BF16, "dT")
        plA = take([128, 528], F32, "plA")
        plB = take([128, 528], F32, "plB")
        Hbs = [take([128, 1024], BF16, f"Hb{i}") for i in range(2)]
        tmps = [take([128, 512], F32, f"tmp{i}") for i in range(2)]
        tmps4 = tmps + [plA.sub(plA.ap[:, 0:512]), plB.sub(plB.ap[:, 0:512])]
        Gpost = take([128, 1024], F32, "Gpost")
        Gln = take([128, 512], F32, "Gln")
        Bln = take([128, 512], F32, "Bln")
        WsT = take([128, 4, 128], BF16, "WsT")
        Wmap = take([128, 4, 128], BF16, "Wmap")
        bsp = take([128, 4, 128], BF16, "bsp")
        wsf = take([128, 128], F32, "wsf")
        cmask = take([128, 128], F32, "cmask")
        invc = take([128, 4, 16], F32, "invc")
        pscale = take([128, 4], F32, "pscale")
        spst = take([128, 512], F32, "spst")
        tailst = spst
        halo_s = take([128, 4, 16], F32, "halo_s")
        t16 = take([128, 16], F32, "t16")
        assert off[0] <= ARENA_BYTES, off[0]
        st_pre = self.statset("pre")
        st_ln = self.statset("ln")
        st_post = self.statset("post")

        self.load_w8(Win, io["w_in"], 1536)
        Win_u = Win_v = Win_p = Win
        self.load_w8(Wout, io["w_out"], 1024)
        self.load_bcast(Gpost, io["nmq"][0], 1024)
        self.load_bcast(Gln, io["ln_g"], 512)
        self.load_bcast(Bln, io["ln_b"], 512)
        self.dma("sp", cmask.ap, io["c_cmask"][:, :], [], [cmask])
        self.dma("sp", invc.ap.rearrange("p g t -> p (g t)"), io["c_invc"].partition_broadcast(128), [], [invc])
        for g in range(4):
            self.dma("pool", Wmap.ap[:, g, :], io["w_map"][g], [], [Wmap])
            self.dma("pool", bsp.ap[0:1, g, :], io["b_sp"][g:g + 1, :], [], [bsp])
            self.dma("sp", pscale.ap[:, g:g + 1], io["p_scale"][g * 128:(g + 1) * 128].rearrange("(p o) -> p o", o=1), [], [pscale], slow=True)
            self.dma("sp", wsf.ap, io["w_sp"][g], [], [wsf])
            b = self.bank()
            self.tr(PS[b][:, 0:128], wsf.ap, c["identf"].ap, [wsf, c["identf"]], [psk(b)])
            self.tt("dve", WsT.ap[:, g, :], PS[b][:, 0:128], cmask.ap, ALU.mult, [psk(b), cmask], [WsT])
        self.dma("sp", spst.ap[0:15, :], io["spool"][:, :], [], [spst])
        b = self.bank()
        for cc in range(4):
            self.tr(PS[b][:, cc * 16:cc * 16 + 15], spst.ap[0:15, cc * 128:(cc + 1) * 128], c["identf"].ap[:15, :15],
                    [spst, c["identf"]], [psk(b)])
        self.cp("act", halo_s.ap[:, :, 0:15], PS[b][:, 0:64].rearrange("p (c t) -> p c t", c=4)[:, :, 0:15], [psk(b)], [halo_s])

        for t in range(4, NTT):
            self.dma("sp", io["X"][t].ap, io["xp"][t * 128:(t + 1) * 128, :], [], [io["X"][t]])
        groups = self.groups + [self.sgroup]

        def stX(gi):
            grp = groups[gi]
            hT, pT, NT, smp = hTs[gi % 2], pTs[gi % 2], grp["NT"], grp["sample"]
            uT = uTs[gi % 2]
            if gi == 0:
                self.memset("pool", pT.ap[:, :, 0:15], 0.0, [pT])
            elif smp:
                self.cp("pool", pT.ap[:, :, 0:15], halo_s.ap[:, :, 0:15], [halo_s], [pT])
            else:
                pprev = pTs[(gi - 1) % 2]
                self.cp("pool", pT.ap[:, :, 0:15], pprev.ap[:, :, 512:527], [pprev], [pT])
            for cc in range(4):
                b = self.bank()
                for k in range(8):
                    self.mm(PS[b][:, 0:NT], Win.ap[:, k, cc * 128:(cc + 1) * 128], hT.ap[:, k, 0:NT], k == 0, k == 7, [Win_u, hT], [psk(b)])
                self.act(uT.ap[:, cc, 0:NT], PS[b][:, 0:NT], AF.Gelu, [psk(b)], [uT])
            for cc in range(4):
                b = self.bank()
                for k in range(8):
                    self.mm(PS[b][:, 0:NT], Win.ap[:, k, 1024 + cc * 128:1024 + (cc + 1) * 128], hT.ap[:, k, 0:NT], k == 0, k == 7,
                            [Win_p, hT], [psk(b)])
                self.cp("act", pT.ap[:, cc, 15:15 + NT], PS[b][:, 0:NT], [psk(b)], [pT])

        def stYv(gi):
            grp = groups[gi]
            hT, tiles, smp = hTs[gi % 2], grp["tiles"], grp["sample"]
            sl = st_ln
            for i, (xb, n, tt) in enumerate(tiles):
                b = self.bank()
                for k in range(8):
                    self.mm(PS[b][:n, :], hT.ap[:, k, i * 128:i * 128 + n], Win.ap[:, k, 512:1024], k == 0, k == 7, [Win_v, hT], [psk(b)])
                self.act(vfs[i].ap[:n, :], PS[b][:n, :], AF.Gelu, [psk(b)], [vfs[i], sl["ss"]], accum_out=sl["ss"].ap[:n, i:i + 1])
                self.act(vn16s[i].ap[:n, :], vfs[i].ap[:n, :], AF.Square, [vfs[i]], [vn16s[i], sl["s2"]], accum_out=sl["s2"].ap[:n, i:i + 1])

        def stY(gi):
            grp = groups[gi]
            hT, tiles, smp = hTs[gi % 2], grp["tiles"], grp["sample"]
            uT = uTs[gi % 2]
            m = len(tiles)
            n0 = tiles[0][1]
            sl = st_ln
            self.ts(sl["mu"].ap[:n0, 0:m], sl["ss"].ap[:n0, 0:m], 1.0 / 512.0, None, ALU.mult, None, [sl["ss"]], [sl["mu"]])
            self.tt("dve", sl["t1"].ap[:n0, 0:m], sl["mu"].ap[:n0, 0:m], sl["mu"].ap[:n0, 0:m], ALU.mult, [sl["mu"]], [sl["t1"]])
            self.stt(sl["t2"].ap[:n0, 0:m], sl["s2"].ap[:n0, 0:m], 1.0 / 512.0, sl["t1"].ap[:n0, 0:m], ALU.mult, ALU.subtract,
                     [sl["s2"], sl["t1"]], [sl["t2"]])
            self.rsqrt(sl, n0, m, (sl["t2"].ap[:n0, 0:m], sl["t2"]))
            for i, (xb, n, tt) in enumerate(tiles):
                vf, vn16 = vfs[i], vn16s[i]
                self.stt(vf.ap[:n, :], vf.ap[:n, :], sl["mu"].ap[:n, i:i + 1], Gln.ap[:n, :], ALU.subtract, ALU.mult, [vf, sl["mu"], Gln], [vf])
                if smp:
                    self.stt(vnf.ap[:n, :], vf.ap[:n, :], sl["y"].ap[:n, i:i + 1], Bln.ap[:n, :], ALU.mult, ALU.add, [vf, sl["y"], Bln], [vnf])
                    self.dma("sp", io["vn_s"][:, :], vnf.ap[:n, :], [vnf], [])
                    self.cp("dve", vn16.ap[:n, :], vnf.ap[:n, :], [vnf], [vn16])
                else:
                    self.stt(vn16.ap[:n, :], vf.ap[:n, :], sl["y"].ap[:n, i:i + 1], Bln.ap[:n, :], ALU.mult, ALU.add, [vf, sl["y"], Bln], [vn16])

        def stYg(gi):
            grp = groups[gi]
            tiles = grp["tiles"]
            uT = uTs[gi % 2]
            for i, (xb, n, tt) in enumerate(tiles):
                vn16 = vn16s[i]
                b2 = self.bank()
                pv = PS[b2][:].rearrange("p (g i) -> p g i", g=4)
                for g in range(4):
                    self.mm(pv[:, g, 0:n], vn16.ap[:n, g * 128:(g + 1) * 128], WsT.ap[:n, g, 0:n], True, False, [vn16, WsT], [psk(b2)], skip=True)
                    self.mm(pv[:, g, 0:n], c["onesrow"].ap[0:1, :], bsp.ap[0:1, g, 0:n], False, True, [c["onesrow"], bsp], [psk(b2)], skip=True)
                c0 = i * 128
                self.tt("dve", abT.ap[:, 0:4, c0:c0 + n], uT.ap[:, :, c0:c0 + n], pv[:, :, 0:n], ALU.mult, [uT, psk(b2)], [abT])

        def stZ(gi):
            grp = groups[gi]
            hT, pT, NT, tiles, smp = hTs[gi % 2], pTs[gi % 2], grp["NT"], grp["tiles"], grp["sample"]
            Wd_ = 15 + NT
            for g in range(4):
                src = pT.ap[:, g, :]
                cur = src
                curb = pT
                bufs = [plA, plB]
                s_ = 1
                for stage in range(g + 1):
                    dst = bufs[stage % 2]
                    lo = 2 * s_ - 1
                    self.tt("pool", dst.ap[:, lo:Wd_], cur[:, lo:Wd_], cur[:, lo - s_:Wd_ - s_], ALU.add, [curb], [dst])
                    cur = dst.ap
                    curb = dst
                    s_ *= 2
                w = POOL_W[g]
                self.stt(dT.ap[:, g, 0:NT], cur[:, 15:15 + NT], 1.0 / w, src[:, 15:15 + NT], ALU.mult, ALU.subtract, [curb, pT], [dT])
                if gi == 0:
                    self.tt("dve", t16.ap[:, :], cur[:, 15:31], invc.ap[:, g, :], ALU.mult, [curb, invc], [t16])
                    self.tt("dve", dT.ap[:, g, 0:16], t16.ap[:, :], src[:, 15:31], ALU.subtract, [t16, pT], [dT])

        def stZm(gi):
            grp = groups[gi]
            hT, pT, NT, tiles, smp = hTs[gi % 2], pTs[gi % 2], grp["NT"], grp["tiles"], grp["sample"]
            for g in range(4):
                b = self.bank()
                self.mm(PS[b][:, 0:NT], Wmap.ap[:, g, :], dT.ap[:, g, 0:NT], True, True, [Wmap, dT], [psk(b)])
                self.act(abT.ap[:, 4 + g, 0:NT], PS[b][:, 0:NT], AF.Copy, [psk(b), pscale], [abT], scale=pscale.ap[:, g:g + 1])
            if smp or gi == 3:
                i_last = len(tiles) - 1
                n = tiles[i_last][1]
                b = self.bank()
                for k in range(8):
                    self.mm(PS[b][:n, :], hT.ap[:, k, i_last * 128:i_last * 128 + n], Win.ap[:, k, 1024:1536], k == 0, k == 7, [Win_p, hT], [psk(b)])
                self.cp("act", tailst.ap[:n, :], PS[b][:n, :], [psk(b)], [tailst])
                if smp:
                    self.dma("sp", io["pool_s"][:, :], tailst.ap[1:16, :], [tailst], [])
                else:
                    self.dma("sp", io["pool_p"][:, :], tailst.ap[113:128, :], [tailst], [])

        def stW(gi):
            tiles = groups[gi]["tiles"]
            for p0 in range(0, len(tiles), 2):
                items = []
                for i in range(p0, min(p0 + 2, len(tiles))):
                    xb, n, tt = tiles[i]
                    banks = [self.bank(), self.bank()]
                    for h, b in enumerate(banks):
                        for k in range(8):
                            self.mm(PS[b][:n, :], abT.ap[:, k, i * 128:i * 128 + n], Wout.ap[:, k, h * 512:(h + 1) * 512], k == 0, k == 7,
                                    [abT, Wout], [psk(b)])
                    items.append((xb, n, banks))
                self.postnorm_multi(items, Gpost, st_post, tmps4, "pool")

        ng = len(groups)
        self.prenorm(groups[0]["tiles"], 0, hTs[0], st_pre, Hbs)
        stX(0)
        stZ(0)
        for gi in range(ng):
            nxt = groups[gi + 1]["tiles"] if gi + 1 < ng else None
            hTn = hTs[(gi + 1) % 2]
            if nxt is not None:
                self.prenorm_front(nxt[0:2], st_pre, Hbs)
            stYv(gi)
            if nxt is not None:
                self.prenorm_back(nxt[0:2], 0, hTn, Hbs, [0, 128])
                if len(nxt) > 2:
                    self.prenorm_front(nxt[2:4], st_pre, Hbs)
            if gi >= 1:
                stW(gi - 1)
            if nxt is not None and len(nxt) > 2:
                self.prenorm_back(nxt[2:4], 0, hTn, Hbs, [256, 384])
            stY(gi)
            if nxt is not None:
                stX(gi + 1)
            stYg(gi)
            stZm(gi)
            if nxt is not None:
                stZ(gi + 1)
        stW(ng - 1)

    def phase_ffn(self, BASE, layer):
        io, c, view, PS, psk = self.io, self.consts, self.view, self.PS, self.psk
        off = [BASE]

        def take(shape, dt, name):
            esz = 4 if dt in (F32, I32) else 2
            free = 1
            for s in shape[1:]:
                free *= s
            b = view(off[0], shape, dt, name)
            off[0] += (free * esz + PG - 1) // PG * PG
            return b

        Wd = [take([128, 1024], BF16, f"Wd{j}") for j in range(NJ)]
        actT = [take([128, 1024], BF16, f"actT{j}") for j in range(NJ)]
        hT = take([128, 8, 1024], BF16, "hTf")
        WgS = [take([128, 8, 256], BF16, f"WgS{i}") for i in range(2)]
        WuS = [take([128, 8, 256], BF16, f"WuS{i}") for i in range(2)]
        Hbs = [take([128, 1024], BF16, f"Hbf{i}") for i in range(2)]
        tmps = [take([128, 512], F32, f"tmpf{i}") for i in range(2)]
        Gpost = take([128, 1024], F32, "Gpostf")
        hTs = take([128, 8, DS], BF16, "hTsf")
        actTs = take([128, NJ, DS], BF16, "actTsf")
        assert off[0] <= ARENA_BYTES, off[0]
        st_pre = self.statset("pre")
        st_pres = self.statset("pres")
        st_post = self.statset("post")
        which = 1 if layer == 0 else 3
        wg, wu, wd = io["w_gate"][layer], io["w_up"][layer], io["w_down"][layer]
        self.load_bcast(Gpost, io["nfq"][layer], 1024)
        X, XS = io["X"], io["XS"]

        def load_group(jg, slot):
            self.dma("pool", WgS[slot].ap, wg[:, jg * 256:(jg + 1) * 256].rearrange("(k p) n -> p k n", p=128), [], [WgS[slot]])
            self.dma("pool", WuS[slot].ap, wu[:, jg * 256:(jg + 1) * 256].rearrange("(k p) n -> p k n", p=128), [], [WuS[slot]])

        gcount = 0

        def down_tiles(tiles, lo, hi):
            for i in range(lo, hi):
                xb, n, tt = tiles[i]
                banks = [self.bank(), self.bank()]
                for h, b in enumerate(banks):
                    for j in range(NJ):
                        self.mm(PS[b][:, :], actT[j].ap[:, i * 128:(i + 1) * 128], Wd[j].ap[:, h * 512:(h + 1) * 512], j == 0, j == NJ - 1,
                                [actT[j], Wd[j]], [psk(b)])
                self.postnorm(xb, n, banks, Gpost, st_post, tmps, "dve")

        all_tiles = [[(X[8 * pas + i], 128, 8 * pas + i) for i in range(8)] for pas in range(2)]
        self.prenorm(all_tiles[0], which, hT, st_pre, Hbs)
        self.prenorm([(XS, DS, None)], which, hTs, st_pres, Hbs)
        for pas in range(2):
            tiles = all_tiles[pas]
            load_group(0, gcount % 2)
            for jg in range(NJ // 2):
                slot = gcount % 2
                gcount += 1
                if jg + 1 < NJ // 2:
                    load_group(jg + 1, gcount % 2)
                if pas == 0:
                    for j in (2 * jg, 2 * jg + 1):
                        self.dma("pool", Wd[j].ap, wd[j * 128:(j + 1) * 128, :], [], [Wd[j]])
                for jj in range(2):
                    j = 2 * jg + jj
                    for blk in range(2):
                        bg, bu = self.bank(), self.bank()
                        cs = slice(blk * 512, (blk + 1) * 512)
                        for k in range(8):
                            self.mm(PS[bg][:, :], WgS[slot].ap[:, k, jj * 128:(jj + 1) * 128], hT.ap[:, k, cs], k == 0, k == 7,
                                    [WgS[slot], hT], [psk(bg)])
                        for k in range(8):
                            self.mm(PS[bu][:, :], WuS[slot].ap[:, k, jj * 128:(jj + 1) * 128], hT.ap[:, k, cs], k == 0, k == 7,
                                    [WuS[slot], hT], [psk(bu)])
                        self.act(actT[j].ap[:, cs], PS[bg][:, :], AF.Silu, [psk(bg)], [actT[j]])
                        self.tt("dve", actT[j].ap[:, cs], actT[j].ap[:, cs], PS[bu][:, :], ALU.mult, [actT[j], psk(bu)], [actT[j]])
                    if pas == 0:
                        b = self.bank()
                        for k in range(8):
                            self.mm(PS[b][:, 0:DS], WgS[slot].ap[:, k, jj * 128:(jj + 1) * 128], hTs.ap[:, k, :], k == 0, k == 7,
                                    [WgS[slot], hTs], [psk(b)], skip=True)
                        for k in range(8):
                            self.mm(PS[b][:, DS:2 * DS], WuS[slot].ap[:, k, jj * 128:(jj + 1) * 128], hTs.ap[:, k, :], k == 0, k == 7,
                                    [WuS[slot], hTs], [psk(b)], skip=True)
                        self.act(actTs.ap[:, j, :], PS[b][:, 0:DS], AF.Silu, [psk(b)], [actTs])
                        self.tt("dve", actTs.ap[:, j, :], actTs.ap[:, j, :], PS[b][:, DS:2 * DS], ALU.mult, [actTs, psk(b)], [actTs])
            for i in range(8):
                down_tiles(tiles, i, i + 1)
                if pas == 0:
                    nx = all_tiles[1]
                    if i % 2 == 0:
                        self.prenorm_front(nx[i:i + 2], st_pre, Hbs)
                    else:
                        self.prenorm_back(nx[i - 1:i + 1], which, hT, Hbs, [(i - 1) * 128, i * 128])
            if pas == 0:
                banks = [self.bank(), self.bank()]
                for h, b in enumerate(banks):
                    for j in range(NJ):
                        self.mm(PS[b][:DS, :], actTs.ap[:, j, :], Wd[j].ap[:, h * 512:(h + 1) * 512], j == 0, j == NJ - 1,
                                [actTs, Wd[j]], [psk(b)])
                self.postnorm(XS, DS, banks, Gpost, st_post, tmps, "dve")

    def phase_C(self, BASE):
        io, c, view, PS, psk = self.io, self.consts, self.view, self.PS, self.psk
        off = [BASE]

        def take(shape, dt, name, at=None):
            esz = 4 if dt in (F32, I32) else 2
            free = 1
            for s in shape[1:]:
                free *= s
            o = off[0] if at is None else at
            b = view(o, shape, dt, name)
            if at is None:
                off[0] += (free * esz + PG - 1) // PG * PG
            return b

        KT = [take([128, S], BF16, f"KT{hp}") for hp in range(8)]
        KTall = view(BASE, [128, 8, S], BF16, "KTall")
        V16 = [take([128, 1024], BF16, f"V16_{t}") for t in range(NTT)]
        ht_off = off[0]
        hT = take([128, 8, 512], BF16, "hTc")
        Hbs = [take([128, 1024], BF16, f"Hbc{i}") for i in range(2)]
        hb0_off = off[0] - 4096
        p1 = off[0]
        w_off = BASE + 90112
        assert p1 + 12288 <= w_off
        W1 = view(w_off, [128, 8, 1024], BF16, "W1")
        W2 = view(w_off + 16384, [128, 8, 1024], BF16, "W2")
        p2 = w_off + 32768
        hT2 = view(p1, [128, 8, 512], BF16, "hTc2")
        KVf = [view(p1 + 8192, [128, 1024], F32, "KVf0"), view(p2, [128, 1024], F32, "KVf1"), view(p2 + 4096, [128, 1024], F32, "KVf2")]
        K16s = [view(p2 + 8192, [128, 1024], BF16, "K16_0")]
        c1_end = p2 + 10240
        off[0] = p1
        oT = take([128, 8, 512], BF16, "oT")
        QTs = [take([128, 512], BF16, f"QTs{i}") for i in range(2)]
        L16 = [take([128, 512], BF16, f"L16_{i}") for i in range(2)]
        assert off[0] <= w_off
        off[0] = p2
        R16 = [take([128, 512], BF16, f"R16_{i}") for i in range(2)]
        Aexp = [take([128, 512], BF16, f"Aexp{i}") for i in range(2)]
        tmp_off = off[0]
        tmps = [take([128, 512], F32, f"tmpc{i}") for i in range(2)]
        Gpost = take([128, 1024], F32, "Gpostc")
        assert off[0] <= ARENA_BYTES - 2560, off[0]
        off[0] = ARENA_BYTES - 2560
        assert off[0] >= c1_end
        QKTs = take([128, 2, 8, DS], BF16, "QKTs")
        Vs16 = take([128, 1024], BF16, "Vs16")
        assert off[0] <= ARENA_BYTES, off[0]
        Lp = [L16, [view(hb0_off, [128, 512], BF16, "L16_2"), view(hb0_off + 1024, [128, 512], BF16, "L16_3")]]
        Ap = [Aexp, [view(hb0_off + 2048, [128, 512], BF16, "Aexp2"), view(hb0_off + 3072, [128, 512], BF16, "Aexp3")]]
        E = view(tmp_off + 2048, [128, 512], F32, "E")
        Ep = [E, view(tmp_off, [128, 512], F32, "E1")]
        qt0 = ht_off
        Kc16 = [view(qt0, [128, 1024], BF16, "Kc16a"), Hbs[0]]
        Vc16 = [view(qt0 + 2048, [128, 1024], BF16, "Vc16a"), Hbs[1], view(tmp_off, [128, 1024], BF16, "Vc16c")]
        KTc = [view(qt0 + 4096, [128, 8, 128], BF16, "KTc0"), view(qt0 + 6144, [128, 8, 128], BF16, "KTc1")]
        st_pre = self.statset("pre")
        st_pres = self.statset("pres")
        st_post = self.statset("post")
        X, XS = io["X"], io["XS"]
        wq, wk, wv = io["w_qkv"][:, 0:1024], io["w_qkv"][:, 1024:2048], io["w_qkv"][:, 2048:3072]
        identb, negT, negones, tri = c["identb"], c["negT"], c["negones"], c["tri"]

        self.load_w8(W1, wk, 1024)
        self.load_w8(W2, wv, 1024)

        kvc = [0]
        kvf_rr = [0]

        def kv_tile(hTb, n, col0, kdst, vdst, kt_out, v16_out, parts=(0, 1)):
            par = kvc[0] % 2
            kvc[0] += 1
            K16 = K16s[0]
            for which_w, Wb, dst in ((0, W1, kdst), (1, W2, vdst)):
                if which_w not in parts:
                    continue
                stg = KVf[kvf_rr[0] % 3]
                kvf_rr[0] += 1
                for h in range(2):
                    b = self.bank()
                    hs = slice(h * 512, (h + 1) * 512)
                    for k in range(8):
                        self.mm(PS[b][:n, :], hTb.ap[:, k, col0:col0 + n], Wb.ap[:, k, hs], k == 0, k == 7, [hTb, Wb], [psk(b)])
                    self.cp("act", stg.ap[:n, hs], PS[b][:n, :], [psk(b)], [stg])
                    if which_w == 0:
                        self.cp("dve", K16.ap[:n, hs], PS[b][:n, :], [psk(b)], [K16])
                    else:
                        self.cp("dve", v16_out[0][:, hs], PS[b][:n, :], [psk(b)], v16_out[1])
                self.dma("sp", dst, stg.ap[:n, :], [stg], [])
            if 0 in parts:
                b = self.bank()
                pv = PS[b][:].bitcast(BF16).rearrange("p (k t) -> p k t", k=8)
                for hp in range(8):
                    self.tr(pv[:, hp, 0:n], K16.ap[:n, hp * 128:(hp + 1) * 128], identb.ap[:n, :n], [K16, identb], [psk(b)])
                self.cp("dve", kt_out[0], pv[:, :, 0:n], [psk(b)], kt_out[1])

        hTl = [hT, hT2]
        self.prenorm([(XS, DS, None)], 2, hT2, st_pres, Hbs)
        self.prenorm(self.groups[0]["tiles"], 2, hTl[0], st_pre, Hbs)
        kv_tile(hT2, DS, 0, io["k_s"][:, :], io["v_s"][:, :], (QKTs.ap[:, 1, :, :], [QKTs]), (Vs16.ap[:DS, :], [Vs16]))
        for g in range(4):
            tiles = self.groups[g]["tiles"]
            nxt = self.groups[g + 1]["tiles"] if g + 1 < 4 else None
            hTn = hTl[(g + 1) % 2]
            if g == 3:
                for parts in ((0,), (1,)):
                    for i, (xb, n, tt) in enumerate(tiles):
                        kv_tile(hTl[g % 2], 128, i * 128, io["k_p"][tt * 128:(tt + 1) * 128, :], io["v_p"][tt * 128:(tt + 1) * 128, :],
                                (KTall.ap[:, :, tt * 128:(tt + 1) * 128], KT), (V16[tt].ap[:, :], [V16[tt]]), parts=parts)
                    if parts == (0,):
                        self.load_w8(W1, wq, 1024)
                continue
            for i, (xb, n, tt) in enumerate(tiles):
                kv_tile(hTl[g % 2], 128, i * 128, io["k_p"][tt * 128:(tt + 1) * 128, :], io["v_p"][tt * 128:(tt + 1) * 128, :],
                        (KTall.ap[:, :, tt * 128:(tt + 1) * 128], KT), (V16[tt].ap[:, :], [V16[tt]]))
                if nxt is not None:
                    if i == 0:
                        self.prenorm_front(nxt[0:2], st_pre, Hbs)
                    elif i == 1:
                        self.prenorm_back(nxt[0:2], 2, hTn, Hbs, [0, 128])
                        self.prenorm_front(nxt[2:4], st_pre, Hbs)
                    elif i == 2:
                        self.prenorm_back(nxt[2:4], 2, hTn, Hbs, [256, 384])

        self.load_w8(W2, io["w_o"], 1024)
        self.load_bcast(Gpost, io["nmq"][1], 1024)
        zpool = [0, 1, 2, 3]
        for QB in range(4):
            tiles = self.groups[QB]["tiles"]
            self.bank_pool = zpool
            def qproj(hp):
                b = self.bank()
                for k in range(8):
                    self.mm(PS[b][:, :], W1.ap[:, k, hp * 128:(hp + 1) * 128], hT.ap[:, k, :], k == 0, k == 7, [W1, hT], [psk(b)])
                self.ts(QTs[hp % 2].ap[:, :], PS[b][:, :], 0.125, None, ALU.mult, None, [psk(b)], [QTs[hp % 2]])

            if QB == 0:
                self.prenorm(tiles, 2, hT, st_pre, Hbs)
                qproj(0)
            kmax = 4 * QB + 3
            self.bank_pool = [0, 1, 2, 3, 6, 7]
            its = [(hp, kb) for hp in range(8) for kb in range(kmax, -1, -1)]
            nit = len(its)
            zb = {}

            def c0_of(kb):
                r = kb - 4 * QB
                return r * 128 if r > 0 else 0

            def stage1(ix):
                hp, kb = its[ix]
                c0 = c0_of(kb)
                zs = (self.bank(), self.bank())
                zb[ix] = zs
                for hh in range(2):
                    rows = slice(hh * 64, (hh + 1) * 64)
                    self.mm(PS[zs[hh]][:, c0:512], KT[hp].ap[rows, kb * 128:(kb + 1) * 128], QTs[hp % 2].ap[rows, c0:512], True, True,
                            [KT[hp], QTs[hp % 2]], [psk(zs[hh])], skip=True)
                if kb == kmax and hp + 1 < 8:
                    qproj(hp + 1)
                for hh in range(2):
                    self.act(Ep[hh].ap[:, c0:512], PS[zs[hh]][:, c0:512], AF.Exp, [psk(zs[hh])], [Ep[hh]])
                for hh in range(2):
                    z = zs[hh]
                    L = Lp[ix % 2][hh]
                    Eb = Ep[hh]
                    self.act(L.ap[:, c0:512], Eb.ap[:, c0:512], AF.Ln, [Eb], [L], bias=1.0)
                    if kb >= 4 * QB:
                        self.tt("dve", L.ap[:, c0:c0 + 128], L.ap[:, c0:c0 + 128], tri.ap[:, :], ALU.mult, [L, tri], [L])

            def stage2(ix):
                hp, kb = its[ix]
                c0 = c0_of(kb)
                first = kb == kmax
                if first:
                    for hh in range(2):
                        self.memset("pool", R16[hh].ap[:, :], 0.0, [R16[hh]])
                for hh in range(2):
                    z = zb[ix][hh]
                    L = Lp[ix % 2][hh]
                    self.mm(PS[z][:, c0:512], negT.ap[:, :], L.ap[:, c0:512], False, first, [negT, L], [psk(z)], skip=True)
                    if not first:
                        self.mm(PS[z][:, c0:512], negones.ap[:, :], R16[hh].ap[:, c0:512], False, True, [negones, R16[hh]], [psk(z)], skip=True)
                for hh in range(2):
                    z = zb[ix][hh]
                    L = Lp[ix % 2][hh]
                    A = Ap[ix % 2][hh]
                    if kb > 0:
                        self.tt("dve", R16[hh].ap[:, c0:512], R16[hh].ap[:, c0:512], L.ap[:, c0:512], ALU.add, [R16[hh], L], [R16[hh]])
                    self.act(A.ap[:, c0:512], PS[z][:, c0:512], AF.Exp, [psk(z)], [A])
                    if kb >= 4 * QB:
                        self.tt("dve", A.ap[:, c0:c0 + 128], A.ap[:, c0:c0 + 128], tri.ap[:, :], ALU.mult, [A, tri], [A])

            def stage3(ix):
                hp, kb = its[ix]
                c0 = c0_of(kb)
                ob = 4 + (hp % 2)
                for hh in range(2):
                    A = Ap[ix % 2][hh]
                    rows = slice(hh * 64, (hh + 1) * 64)
                    cols = slice(hp * 128 + hh * 64, hp * 128 + (hh + 1) * 64)
                    self.mm(PS[ob][rows, c0:512], V16[kb].ap[:, cols], A.ap[:, c0:512], kb == kmax, kb == 0,
                            [V16[kb], A], [psk(ob)], skip=True)
                if kb == 0:
                    self.cp("dve", oT.ap[:, hp, :], PS[ob][:, :], [psk(ob)], [oT])

            for s_ in range(nit + 2):
                if s_ < nit:
                    stage1(s_)
                if 0 <= s_ - 1 < nit:
                    stage2(s_ - 1)
                if 0 <= s_ - 2 < nit:
                    stage3(s_ - 2)
            self.bank_pool = zpool
            nxt_t = self.groups[QB + 1]["tiles"] if QB + 1 < 4 else None
            if nxt_t is not None:
                self.prenorm_front(nxt_t[0:2], st_pre, Hbs)
            self.bank_pool = list(range(8))
            self.bank_rr = 0
            tmps4 = tmps + [view(p2, [128, 512], F32, "R16asF32"), view(p2 + 2048, [128, 512], F32, "AexpAsF32")]
            for p0 in range(0, 4, 2):
                items = []
                for i in range(p0, p0 + 2):
                    xb, n, tt = tiles[i]
                    banks = [self.bank(), self.bank()]
                    for h, b in enumerate(banks):
                        for hp in range(8):
                            self.mm(PS[b][:, :], oT.ap[:, hp, i * 128:(i + 1) * 128], W2.ap[:, hp, h * 512:(h + 1) * 512], hp == 0, hp == 7,
                                    [oT, W2], [psk(b)])
                    items.append((xb, n, banks))
                if nxt_t is not None:
                    if p0 == 0:
                        self.prenorm_back(nxt_t[0:2], 2, hT, Hbs, [0, 128])
                        self.prenorm_front(nxt_t[2:4], st_pre, Hbs)
                    else:
                        self.prenorm_back(nxt_t[2:4], 2, hT, Hbs, [256, 384])
                self.postnorm_multi(items, Gpost, st_post, tmps4, "dve")
            if nxt_t is not None:
                qproj(0)
            self.bank_pool = zpool

        self.bank_pool = zpool
        self.prenorm([(XS, DS, None)], 2, hT, st_pres, [Hbs[0]])
        b = self.bank()
        pvq = PS[b][:, 0:128].rearrange("p (h q) -> p h q", h=8)
        for hp in range(8):
            for k in range(8):
                self.mm(pvq[:, hp, :], W1.ap[:, k, hp * 128:(hp + 1) * 128], hT.ap[:, k, 0:DS], k == 0, k == 7, [W1, hT], [psk(b)], skip=True)
        self.act(QKTs.ap[:, 0, :, :], pvq, AF.Copy, [psk(b)], [QKTs], scale=0.125)
        QZb = R16[1].sub(R16[1].ap[:, 0:256].rearrange("p (a b q) -> p a b q", a=8, b=2))
        self.memset("pool", R16[1].ap[:, 0:256], 0.0, [R16[1]])
        for hh in range(2):
            rows = slice(hh * 64, (hh + 1) * 64)
            self.cp("pool", QZb.ap[rows, :, hh, :], QKTs.ap[rows, 0, :, :], [QKTs, R16[1]], [R16[1]])
        ob = 4
        self.memset("dve", PS[ob][:, 0:256], 0.0, [psk(ob)])
        self.memset("pool", R16[0].ap[:, 0:256], 0.0, [R16[0]])
        blocks = [None] + list(range(15, -1, -1))
        nblk = len(blocks)
        zb = {}
        Ls, As_ = L16, Aexp
        HW = 16 * DS

        def s_kv(ix):
            kb = blocks[ix]
            if kb is None:
                return DS, QKTs.ap[:, 1, :, :], QKTs, Vs16
            return 128, KTc[ix % 2].ap, KTc[ix % 2], Vc16[ix % 3]

        def s_stage0(ix):
            kb = blocks[ix]
            if kb is None:
                return
            kc, vc, ktc = Kc16[ix % 2], Vc16[ix % 3], KTc[ix % 2]
            self.dma("pool", kc.ap[:, :], io["ck"][kb * 128:(kb + 1) * 128, :], [], [kc])
            self.dma("pool", vc.ap[:, :], io["cv"][kb * 128:(kb + 1) * 128, :], [], [vc])
            b_ = self.bank()
            pv = PS[b_][:].bitcast(BF16).rearrange("p (k t) -> p k t", k=8)
            for hp in range(8):
                self.tr(pv[:, hp, :], kc.ap[:, hp * 128:(hp + 1) * 128], identb.ap[:, :], [kc, identb], [psk(b_)])
            self.cp("dve", ktc.ap[:, :, :], pv[:, :, :], [psk(b_)], [ktc])

        def s_stage1(ix):
            nk, ktap, ktbuf, vbuf = s_kv(ix)
            z = self.bank()
            zb[ix] = z
            self.memset("dve", PS[z][:, 0:HW], 0.0, [psk(z)])
            for h in range(16):
                self.mm(PS[z][:nk, h * DS:(h + 1) * DS], ktap[:, h // 2, 0:nk], QZb.ap[:, h // 2, h % 2, :], False, False,
                        [ktbuf, R16[1]], [psk(z)], skip=True)
            self.act(E.ap[:nk, 0:HW], PS[z][:nk, 0:HW], AF.Exp, [psk(z)], [E])
            L = Ls[ix % 2]
            self.act(L.ap[:nk, 0:HW], E.ap[:nk, 0:HW], AF.Ln, [E], [L], bias=1.0)
            if blocks[ix] is None:
                lv = L.ap[:nk, 0:HW].rearrange("p (h q) -> p h q", h=16)
                self.tt("dve", lv, lv, tri.ap[:DS, :DS].unsqueeze(1).to_broadcast([DS, 16, DS]), ALU.mult, [L, tri], [L])

        def s_stage2(ix):
            nk, ktap, ktbuf, vbuf = s_kv(ix)
            z = zb[ix]
            L = Ls[ix % 2]
            A = As_[ix % 2]
            self.mm(PS[z][:nk, 0:HW], negT.ap[:nk, :nk], L.ap[:nk, 0:HW], False, ix == 0, [negT, L], [psk(z)], skip=True)
            if ix > 0:
                self.mm(PS[z][:nk, 0:HW], negones.ap[:, :nk], R16[0].ap[:, 0:HW], False, True, [negones, R16[0]], [psk(z)], skip=True)
            if ix + 1 < nblk:
                self.tt("dve", R16[0].ap[:nk, 0:HW], R16[0].ap[:nk, 0:HW], L.ap[:nk, 0:HW], ALU.add, [R16[0], L], [R16[0]])
            self.act(A.ap[:nk, 0:HW], PS[z][:nk, 0:HW], AF.Exp, [psk(z)], [A])
            if blocks[ix] is None:
                av = A.ap[:nk, 0:HW].rearrange("p (h q) -> p h q", h=16)
                self.tt("dve", av, av, tri.ap[:DS, :DS].unsqueeze(1).to_broadcast([DS, 16, DS]), ALU.mult, [A, tri], [A])

        def s_stage3(ix):
            nk, ktap, ktbuf, vbuf = s_kv(ix)
            A = As_[ix % 2]
            for h in range(16):
                hp = h // 2
                self.mm(PS[ob][:, h * DS:(h + 1) * DS], vbuf.ap[:nk, hp * 128:(hp + 1) * 128], A.ap[:nk, h * DS:(h + 1) * DS], False, False,
                        [vbuf, A], [psk(ob)], skip=True)

        for s_ in range(nblk + 3):
            if 0 <= s_ - 3 < nblk:
                s_stage3(s_ - 3)
            if 0 <= s_ - 2 < nblk:
                s_stage2(s_ - 2)
            if 0 <= s_ - 1 < nblk:
                s_stage1(s_ - 1)
            if s_ < nblk:
                s_stage0(s_)
        ov = PS[ob][:, 0:HW].rearrange("p (a b q) -> p a b q", a=8, b=2)
        for hh in range(2):
            rows = slice(hh * 64, (hh + 1) * 64)
            self.cp("dve", oT.ap[rows, :, 0:DS], ov[rows, :, hh, :], [psk(ob)], [oT])
        banks = [self.bank(), self.bank()]
        for h, b in enumerate(banks):
            for hp in range(8):
                self.mm(PS[b][:DS, :], oT.ap[:, hp, 0:DS], W2.ap[:, hp, h * 512:(h + 1) * 512], hp == 0, hp == 7, [oT, W2], [psk(b)])
        self.postnorm(XS, DS, banks, Gpost, st_post, tmps, "pool")
        self.bank_pool = list(range(8))


def _consts():
    i = np.arange(128)
    c = {}
    c["c_ident"] = np.eye(128, dtype=np.float32)
    c["c_negT"] = -(i[:, None] >= i[None, :]).astype(np.float32)
    c["c_negones"] = -np.ones((128, 128), np.float32)
    c["c_tri"] = (i[:, None] < i[None, :]).astype(np.float32)
    c["c_cmask"] = ((i[:, None] // 64) <= (i[None, :] // 64)).astype(np.float32)
    inv = np.zeros((4, 16), np.float32)
    for g, w in enumerate(POOL_W):
        inv[g] = 1.0 / np.minimum(w, np.arange(16) + 1)
    c["c_invc"] = inv.reshape(64)
    c["c_onesrow"] = np.ones((1, 128), np.float32)
    return c


def make_in_maps(inp):
    f = lambda a: np.ascontiguousarray(np.asarray(a, dtype=np.float32))
    shared = {
        "nmp": f(inp["norm_mix_pre"]), "nmq": f(inp["norm_mix_post"]), "nfp": f(inp["norm_ffn_pre"]), "nfq": f(inp["norm_ffn_post"]),
        "w_in": f(inp["w_in_ab"])[0], "ln_g": f(inp["ln_v_g"])[0], "ln_b": f(inp["ln_v_b"])[0],
        "w_sp": f(inp["w_spatial"])[0], "b_sp": f(inp["b_spatial"])[0], "w_map": f(inp["w_pool_map"])[0],
        "p_scale": f(inp["pool_scale"])[0], "w_out": f(inp["w_out_ab"])[0], "w_qkv": f(inp["w_qkv"])[0], "w_o": f(inp["w_o_sb"])[0],
        "w_gate": f(inp["w_gate"]), "w_up": f(inp["w_up"]), "w_down": f(inp["w_down"]),
    }
    shared.update(_consts())
    xp = f(inp["x_prompt"]); xs = f(inp["x_sample"]); sp = f(inp["state_pool"]); ck = f(inp["cache_k"]); cv = f(inp["cache_v"])
    maps = []
    for b in range(8):
        m = dict(shared)
        m["xp"] = xp[b]
        m["xs"] = xs[b]
        m["spool"] = sp[0, b]
        m["ck"] = ck[0, b].reshape(S, D)
        m["cv"] = cv[0, b].reshape(S, D)
        maps.append(m)
    return maps


def assemble(res):
    g = lambda k: np.stack([np.asarray(r[k], dtype=np.float32) for r in res])
    y_p = g("y_p")
    y_s = g("y_s")
    pool_p = g("pool_p")[None]
    k_p = g("k_p").reshape(1, 8, S, 16, 64)
    v_p = g("v_p").reshape(1, 8, S, 16, 64)
    pool_s = g("pool_s")[None]
    k_s = g("k_s").reshape(1, 8, DS, 16, 64)
    v_s = g("v_s").reshape(1, 8, DS, 16, 64)
    vn_s = g("vn_s")[None]
    return (y_p, y_s, pool_p, k_p, v_p, pool_s, k_s, v_s, vn_s)


def kernel(**inputs):
    nc = Builder().build()
    maps = make_in_maps(inputs)
    res = run_bass_kernel_spmd(nc, maps, core_ids=list(range(8)))
    return assemble(res.results)
```
